# Optimizing a Trainium2 kernel written in Bass

```python
import jax, jax.numpy as jnp
from jax import lax
import numpy as np

D_MODEL = 1024
BATCH = 8
SEQ = 2048
DEPTH = 1

CHUNK = 64
Q_BLOCK = 128
HEAD_DIM = 64
N_HEADS_SB = 8
N_HEADS_FOX = 8
WIDTH_SB = N_HEADS_SB * HEAD_DIM
WIDTH_FOX = N_HEADS_FOX * HEAD_DIM
IN_COLS = 3 * WIDTH_SB + 3 * WIDTH_FOX + N_HEADS_FOX
D_FF = 2816
RMS_EPS = 1e-6
ATTN_SCALE = HEAD_DIM ** -0.5
FORGET_BIAS_MEAN = 2.0

kernel_name = "macaron_stickbreak_forgetting_gated_block"


def rms_norm(x, g):
    xf = x.astype(jnp.float32)
    y = xf * lax.rsqrt(jnp.mean(xf * xf, axis=-1, keepdims=True) + RMS_EPS)
    return (y * g.astype(jnp.float32)).astype(x.dtype)


def swiglu(h, w_gate, w_up, w_down):
    return (jax.nn.silu(h @ w_gate) * (h @ w_up)) @ w_down


def split_heads(t, n_heads):
    b, s, _ = t.shape
    return t.reshape(b, s, n_heads, HEAD_DIM).transpose(0, 2, 1, 3)


def merge_heads(t):
    b, h, s, d = t.shape
    return t.transpose(0, 2, 1, 3).reshape(b, s, h * d)


def stick_breaking_attention(q, k, v):
    seq = q.shape[2]
    outs = []
    for i in range(seq // Q_BLOCK):
        q0 = i * Q_BLOCK
        k_end = q0 + Q_BLOCK
        z = jnp.einsum('bhqd,bhkd->bhqk', q[:, :, q0:k_end], k[:, :, :k_end]).astype(jnp.float32) * ATTN_SCALE
        t_pos = q0 + jnp.arange(Q_BLOCK)[:, None]
        s_pos = jnp.arange(k_end)[None, :]
        strict = s_pos < t_pos
        log_not_beta = jnp.where(strict, jax.nn.log_sigmoid(-z), 0.0)
        between = lax.cumsum(log_not_beta, axis=3, reverse=True) - log_not_beta
        weights = jnp.where(strict, jnp.exp(jax.nn.log_sigmoid(z) + between), 0.0)
        outs.append(jnp.einsum('bhqk,bhkd->bhqd', weights.astype(v.dtype), v[:, :, :k_end]))
    return jnp.concatenate(outs, axis=2)


def forgetting_attention(q, k, v, log_f_cum):
    seq = q.shape[2]
    outs = []
    for i in range(seq // Q_BLOCK):
        q0 = i * Q_BLOCK
        k_end = q0 + Q_BLOCK
        logits = jnp.einsum('bhqd,bhkd->bhqk', q[:, :, q0:k_end], k[:, :, :k_end]).astype(jnp.float32) * ATTN_SCALE
        logits = logits + log_f_cum[:, :, q0:k_end, None] - log_f_cum[:, :, None, :k_end]
        t_pos = q0 + jnp.arange(Q_BLOCK)[:, None]
        s_pos = jnp.arange(k_end)[None, :]
        logits = jnp.where(s_pos <= t_pos, logits, -jnp.inf)
        probs = jax.nn.softmax(logits, axis=-1)
        outs.append(jnp.einsum('bhqk,bhkd->bhqd', probs.astype(v.dtype), v[:, :, :k_end]))
    return jnp.concatenate(outs, axis=2)


def setup_inputs(seed: int = 0) -> dict:
    key = jax.random.key(seed)
    ks = jax.random.split(key, 20)
    f32 = jnp.float32

    def dense(k, fan_in, fan_out):
        return jax.random.normal(k, (DEPTH, fan_in, fan_out), f32) * fan_in ** -0.5

    def gain(k, shape):
        return 1.0 + 0.02 * jax.random.normal(k, shape, f32)

    return {
        'x': jax.random.normal(ks[0], (BATCH, SEQ, D_MODEL), f32),
        'norm_ffn1': gain(ks[1], (DEPTH, D_MODEL)),
        'w_ffn1_gate': dense(ks[2], D_MODEL, D_FF),
        'w_ffn1_up': dense(ks[3], D_MODEL, D_FF),
        'w_ffn1_down': dense(ks[4], D_FF, D_MODEL),
        'norm_mix': gain(ks[5], (DEPTH, D_MODEL)),
        'w_in': dense(ks[6], D_MODEL, IN_COLS),
        'b_forget': FORGET_BIAS_MEAN + 0.1 * jax.random.normal(ks[7], (DEPTH, N_HEADS_FOX), f32),
        'w_gate': dense(ks[8], D_MODEL, 2 * D_MODEL),
        'b_gate': 0.02 * jax.random.normal(ks[9], (DEPTH, 2 * D_MODEL), f32),
        'w_up_a': dense(ks[10], WIDTH_SB, D_MODEL),
        'w_up_b': dense(ks[11], WIDTH_FOX, D_MODEL),
        'w_out': dense(ks[12], D_MODEL, D_MODEL),
        'norm_ffn2': gain(ks[13], (DEPTH, D_MODEL)),
        'w_ffn2_gate': dense(ks[14], D_MODEL, D_FF),
        'w_ffn2_up': dense(ks[15], D_MODEL, D_FF),
        'w_ffn2_down': dense(ks[16], D_FF, D_MODEL),
        'norm_final': gain(ks[17], (D_MODEL,)),
    }


def reference(x, norm_ffn1, w_ffn1_gate, w_ffn1_up, w_ffn1_down, norm_mix, w_in, b_forget,
              w_gate, b_gate, w_up_a, w_up_b, w_out, norm_ffn2, w_ffn2_gate, w_ffn2_up,
              w_ffn2_down, norm_final):
    splits = np.cumsum([WIDTH_SB, WIDTH_SB, WIDTH_SB, WIDTH_FOX, WIDTH_FOX, WIDTH_FOX]).tolist()
    for l in range(DEPTH):
        x = x + 0.5 * swiglu(rms_norm(x, norm_ffn1[l]), w_ffn1_gate[l], w_ffn1_up[l], w_ffn1_down[l])

        h = rms_norm(x, norm_mix[l])
        proj = h @ w_in[l]
        q_a, k_a, v_a, q_b, k_b, v_b, f_logit = jnp.split(proj, splits, axis=-1)

        y_a = merge_heads(stick_breaking_attention(
            split_heads(q_a, N_HEADS_SB), split_heads(k_a, N_HEADS_SB), split_heads(v_a, N_HEADS_SB)))

        log_f = jax.nn.log_sigmoid((f_logit + b_forget[l]).astype(jnp.float32))
        log_f_cum = jnp.cumsum(log_f, axis=1).transpose(0, 2, 1)
        y_b = merge_heads(forgetting_attention(
            split_heads(q_b, N_HEADS_FOX), split_heads(k_b, N_HEADS_FOX), split_heads(v_b, N_HEADS_FOX),
            log_f_cum))

        gates = jax.nn.sigmoid(h @ w_gate[l] + b_gate[l])
        g_a, g_b = jnp.split(gates, 2, axis=-1)
        mixed = g_a * (y_a @ w_up_a[l]) + g_b * (y_b @ w_up_b[l])
        x = x + mixed @ w_out[l]

        x = x + 0.5 * swiglu(rms_norm(x, norm_ffn2[l]), w_ffn2_gate[l], w_ffn2_up[l], w_ffn2_down[l])
    return rms_norm(x, norm_final)
```

```python
import numpy as np
from contextlib import ExitStack
import concourse.bass as bass
import concourse.mybir as mybir
from concourse.bass_utils import run_bass_kernel_spmd

F32 = mybir.dt.float32
BF16 = mybir.dt.bfloat16
AF = mybir.ActivationFunctionType
ALU = mybir.AluOpType

T = 2048
D = 1024
DFF = 2816
NEG = -30000.0
ENGS = ("pe", "act", "dve", "pool", "sp")


class Op:
    __slots__ = ("eng", "idx", "fn", "deps", "dma", "sig", "needed", "name", "_slot_prev", "_barriered")

    def __init__(self, eng, idx, fn, dma, name):
        self.eng = eng
        self.idx = idx
        self.fn = fn
        self.dma = dma
        self.deps = []
        self.sig = None
        self.needed = False
        self.name = name
        self._slot_prev = 0
        self._barriered = False


class Sched:
    def __init__(self, nc, same_engine_sync=True, dma_slots=8):
        self.nc = nc
        self.ops = {e: [] for e in ENGS}
        self.last_w = {}
        self.readers = {}
        self.same_engine_sync = same_engine_sync
        self.dma_slots = dma_slots

    def add(self, eng, fn, reads=(), writes=(), dma=False, name="", extra_deps=(), dur=0.0):
        op = Op(eng, len(self.ops[eng]), fn, dma, name)
        deps = []

        def want(o, kind):
            if o is None or o is op:
                return
            if o.eng == eng and not o.dma and not dma:
                if eng == "pe":
                    return
                if kind != "raw" or not self.same_engine_sync:
                    return
            deps.append(o)

        for k in reads:
            want(self.last_w.get(k), "raw")
        for k in writes:
            want(self.last_w.get(k), "waw")
            for r in self.readers.get(k, ()):
                want(r, "war")
        for o in extra_deps:
            want(o, "raw")
        best = {}
        out = []
        for o in deps:
            if o.dma:
                if o not in out:
                    out.append(o)
            else:
                b = best.get(o.eng)
                if b is None or o.idx > b.idx:
                    best[o.eng] = o
        out.extend(best.values())
        op.deps = out
        for o in out:
            o.needed = True
        for k in writes:
            self.last_w[k] = op
            self.readers[k] = []
        for k in reads:
            lst = self.readers.setdefault(k, [])
            if not dma:
                lst[:] = [r for r in lst if r.dma or r.eng != eng]
            lst.append(op)
        self.ops[eng].append(op)
        return op

    def barrier(self):
        lasts = []
        for e in ENGS:
            ol = self.ops[e]
            if ol:
                for o in reversed(ol):
                    if not o.dma and o.fn is not None:
                        lasts.append(o)
                        break
                lasts.extend(o for o in ol if o.dma and not o._barriered)
        for o in lasts:
            if o.dma:
                o._barriered = True
        for e in ENGS:
            self.add(e, None, extra_deps=list(lasts), name="barrier")

    def emit(self, sems):
        for e in ENGS:
            cnt = 0
            slot_uses = [0] * self.dma_slots
            nd = 0
            for o in self.ops[e]:
                if o.dma:
                    s = nd % self.dma_slots
                    nd += 1
                    slot_uses[s] += 1
                    o.sig = (sems["dma_%s_%d" % (e, s)], 16 * slot_uses[s], 16)
                    o._slot_prev = 16 * (slot_uses[s] - 1)
                elif o.needed and o.fn is not None:
                    cnt += 1
                    o.sig = (sems[e], cnt, 1)
                elif o.needed and o.fn is None:
                    raise RuntimeError("dependency on pure-wait op")
        engobj = {"pe": "tensor", "act": "scalar", "dve": "vector", "pool": "gpsimd", "sp": "sync"}

        def run(e, eng):
            waited = {}
            for o in self.ops[e]:
                ws = []
                for d in o.deps:
                    sem, val, _ = d.sig
                    ws.append((sem, val))
                if o.dma and o._slot_prev > 0:
                    ws.append((o.sig[0], o._slot_prev))
                ws.sort(key=lambda x: -x[1])
                for sem, val in ws:
                    key = id(sem)
                    if waited.get(key, 0) >= val:
                        continue
                    waited[key] = val
                    eng.wait_ge(sem, val)
                if o.fn is not None:
                    ins = o.fn(eng)
                    if o.sig is not None:
                        ins.then_inc(o.sig[0], o.sig[2])

        with self.nc.Block() as block:
            for e in ENGS:
                if not self.ops[e]:
                    continue
                deco = getattr(block, engobj[e])

                def make(e=e):
                    def body(eng):
                        run(e, eng)
                    return body

                deco(make())


def _fsize(ap):
    n = 1
    for d in ap.shape[1:]:
        n *= d
    return n


class Recorder:
    def __init__(self):
        self.ops = []

    def add(self, eng, fn, reads=(), writes=(), dma=False, name="", extra_deps=(), dur=0.0):
        assert not dma
        self.ops.append((eng, fn, tuple(reads), tuple(writes), dur))


class Builder:
    def __init__(self, stage=3, dbg=False):
        self.stage = stage
        self.dbg = dbg
        self.nc = bass.Bass("TRN2", target_bir_lowering=False)
        self.bank_rr = 0

    def dram_in(self, name, shape, dt=F32):
        return self.nc.dram_tensor(name, list(shape), dt, kind="ExternalInput").ap()

    def mm(self, out, lhsT, rhs, start, stop, reads, writes):
        self.S.add("pe", lambda t, o=out, l=lhsT, r=rhs, a=start, b=stop: t.matmul(o, lhsT=l, rhs=r, start=a, stop=b),
                   reads=reads, writes=writes, dur=0.03 + out.shape[-1] * 0.00043)

    def act(self, out, in_, func, reads, writes, bias=None, scale=None):
        kw = {}
        if bias is not None:
            kw["bias"] = bias
        if scale is not None:
            kw["scale"] = scale
        self.S.add("act", lambda s, o=out, i=in_, f=func, kw=kw: s.activation(out=o, in_=i, func=f, **kw),
                   reads=reads, writes=writes, dur=(_fsize(out) + 335) / 1200.0)

    def tt(self, out, in0, in1, op, reads, writes, eng="dve"):
        self.S.add(eng, lambda v, o=out, a=in0, b=in1, op=op: v.tensor_tensor(out=o, in0=a, in1=b, op=op),
                   reads=reads, writes=writes, dur=0.15 + _fsize(out) / 960.0)

    def stt(self, out, in0, scalar, in1, op0, op1, reads, writes):
        self.S.add("dve", lambda v, o=out, a=in0, s=scalar, b=in1, p0=op0, p1=op1:
                   v.scalar_tensor_tensor(out=o, in0=a, scalar=s, in1=b, op0=p0, op1=p1), reads=reads, writes=writes)

    def copy(self, out, in_, reads, writes, eng="dve"):
        self.S.add(eng, lambda v, o=out, i=in_: v.tensor_copy(out=o, in_=i), reads=reads, writes=writes,
                   dur=0.15 + _fsize(out) / 960.0)

    def dma(self, eng, out, in_, reads=(), writes=()):
        return self.S.add(eng, lambda g, o=out, i=in_: g.dma_start(out=o, in_=i), reads=reads, writes=writes, dma=True)

    def bank(self, lo=0, hi=4):
        b = lo + self.bank_rr % (hi - lo)
        self.bank_rr += 1
        return b

    def psb(self, b, rows=slice(0, 128), c0=0, c1=512):
        return self.ps[rows, b * 512 + c0:b * 512 + c1]

    def carve(self, off, nbytes, dt, pattern=None, **kw):
        a = self.R[:, off // 2:(off + nbytes) // 2]
        if dt == F32:
            a = a.bitcast(F32)
        if pattern:
            a = a.rearrange(pattern, **kw)
        return a

    def build(self):
        nc = self.nc
        din = self.dram_in
        self.d_xT = din("xT", [D, T])
        self.d_wg = [din("wg1", [11, 128, 2048]), din("wg2", [11, 128, 2048])]
        self.d_wu = [din("wu1", [11, 128, 2048]), din("wu2", [11, 128, 2048])]
        self.d_wd = [din("wd1", [22, 128, 1024]), din("wd2", [22, 128, 1024])]
        self.d_win = din("win", [24, 128, 1024])
        self.d_wf = din("wf", [128, 64])
        self.d_wgate = din("wgate", [16, 128, 1024])
        self.d_wupa = din("wupa", [128, 4096])
        self.d_wupb = din("wupb", [128, 4096])
        self.d_wout = din("wout", [8, 128, 1024])
        self.d_gvec = din("gvec", [128, 32])
        self.d_bgate = din("bgate", [128, 16])
        self.d_bfor = din("bfor", [8, 1])
        self.d_cmat = din("cmat", [128, 1664])
        self.d_caug = din("caug", [3, 64, 2048])
        self.d_out = nc.dram_tensor("outT", [D, T], F32, kind="ExternalOutput").ap()
        if self.dbg:
            self.d_dbg = nc.dram_tensor("dbgy", [2, 128, 4 * T], BF16, kind="ExternalOutput").ap()

        with ExitStack() as es:
            def sbuf(name, shape, dt):
                return es.enter_context(nc.sbuf_tensor(name, shape, dt))
            self.xT = sbuf("xT_sb", [128, 8, T], F32)
            self.hT = sbuf("hT_sb", [128, 8, T], BF16)
            self.R = sbuf("R_sb", [128, 54272], BF16)
            self.cm = sbuf("cm_sb", [128, 1664], BF16)
            self.gvec = sbuf("gvec_sb", [128, 32], F32)
            self.bgate = sbuf("bgate_sb", [128, 16], F32)
            self.bfor = sbuf("bfor_sb", [8, 1], F32)
            self.wf = sbuf("wf_sb", [128, 64], BF16)
            self.negb = sbuf("negb_sb", [8, 1], F32)
            self.ps = es.enter_context(nc.psum_tensor("ps", [128, 4096], F32))
            sems = {}
            for e in ENGS:
                sems[e] = es.enter_context(nc.semaphore("c_" + e))
                for i in range(8):
                    sems["dma_%s_%d" % (e, i)] = es.enter_context(nc.semaphore("d_%s_%d" % (e, i)))
            self.S = Sched(nc)
            self.sq_i = 0
            self.p2cnt = 0
            self.ident = self.cm[:, 0:128]
            self.maskS = self.cm[:, 128:256]
            self.maskC = self.cm[:, 256:384]
            self.triN = self.cm[:, 384:512]
            self.meanm = self.cm[:, 512:640]
            self.lhsR = self.cm[:, 640:1664].rearrange("p (a b) -> p a b", b=64)

            self.program()
            self.S.emit(sems)
        return nc

    def program(self):
        S = self.S
        xsrc = self.d_xT.rearrange("(c p) t -> p c t", p=128)
        for tt in range(4):
            self.dma("sp" if tt % 2 == 0 else "act", self.xT[:, :, tt * 512:(tt + 1) * 512], xsrc[:, :, tt * 512:(tt + 1) * 512],
                     writes=[("x", c, tt) for c in range(8)])
        self.dma("pool", self.cm[:], self.d_cmat, writes=["cm"])
        self.dma("sp", self.gvec[:], self.d_gvec, writes=["gvec"])
        self.dma("sp", self.bgate[:], self.d_bgate, writes=["bgate"])
        self.dma("sp", self.bfor[:], self.d_bfor, writes=["bfor"])

        full = self.stage >= 3
        self.ffn(0, tail=(lambda tt: self.norm_tt(tt, 1, "h")) if self.stage >= 2 else None)
        if self.stage >= 2:
            self.mixer()
        if full:
            S.barrier()
            self.ffn(1, tail=lambda tt: self.norm_tt(tt, 3, "final"))
        else:
            S.barrier()
            dst = self.d_out.rearrange("(c p) t -> p c t", p=128)
            for tt in range(4):
                ts = slice(tt * 512, (tt + 1) * 512)
                self.dma("sp", dst[:, :, ts], self.xT[:, :, ts], reads=[("x", c, tt) for c in range(8)])
        alld = [o for o in S.ops["sp"] if o.dma]
        S.add("sp", None, extra_deps=alld, name="final")

    def norm_scratch(self, off=77824):
        sq = self.carve(off, 2048, BF16, "p (a b) -> p a b", b=512)
        lnv = self.carve(off + 2048, 2048, F32)
        rstd = self.carve(off + 4096, 8192, F32, "p (a b) -> p a b", b=512)
        return sq, lnv, rstd

    def norm_tt(self, tt, gidx, mode):
        sq, lnv, rstd = self.norm_scratch()
        ts = slice(tt * 512, (tt + 1) * 512)
        b = self.bank(4, 8)
        for c in range(8):
            i = self.sq_i
            self.sq_i += 1
            self.act(sq[:, i % 2, :], self.xT[:, c, ts], AF.Square, reads=[("x", c, tt)], writes=[("sq", i % 2)])
            self.mm(self.psb(b), self.meanm, sq[:, i % 2, :], c == 0, c == 7,
                    reads=[("sq", i % 2), "cm"], writes=[("ps", b)])
        self.act(lnv, self.psb(b), AF.Ln, reads=[("ps", b)], writes=["lnv"], bias=1e-6)
        self.act(rstd[:, tt, :], lnv, AF.Exp, reads=["lnv"], writes=[("rstd", tt)], scale=-0.5)
        for c in range(8):
            if mode == "h":
                self.stt(self.hT[:, c, ts], self.xT[:, c, ts], self.gvec[:, gidx * 8 + c:gidx * 8 + c + 1], rstd[:, tt, :],
                         ALU.mult, ALU.mult, reads=[("x", c, tt), ("rstd", tt), "gvec"], writes=[("h", c, tt)])
            else:
                xs = self.xT[:, c, ts]
                self.stt(xs, xs, self.gvec[:, gidx * 8 + c:gidx * 8 + c + 1], rstd[:, tt, :], ALU.mult, ALU.mult,
                         reads=[("x", c, tt), ("rstd", tt), "gvec"], writes=[("x", c, tt)])
        if mode == "final":
            dst = self.d_out.rearrange("(c p) t -> p c t", p=128)
            self.dma("sp", dst[:, :, ts], self.xT[:, :, ts], reads=[("x", c, tt) for c in range(8)])

    def norm_to_h(self, gidx, off=77824):
        for tt in range(4):
            self.norm_tt(tt, gidx, "h")

    def ffn(self, which, tail=None, skip_norm=False):
        S = self.S
        aT = self.carve(0, 24576, BF16, "p (a b) -> p a b", b=T)
        wgu = self.carve(24576, 24576, BF16, "p (n g j k f) -> p n g j k f", n=3, g=2, j=2, k=8)
        wd = self.carve(49152, 24576, BF16, "p (n j d) -> p n j d", n=2, j=6)
        st = self.carve(73728, 4096, F32, "p (a b) -> p a b", b=512)
        if not skip_norm:
            self.norm_to_h(0 if which == 0 else 2)
        quarters = [(0, 3), (3, 6), (6, 9), (9, 11)]
        gu_i = 0
        for q, (g0, g1) in enumerate(quarters):
            nj = 2 * (g1 - g0)
            wb = q % 2
            for g in range(g0, g1):
                buf = g % 3
                self.dma("pool", wgu[:, buf, 0].rearrange("p j k f -> p (j k f)"), self.d_wg[which][g],
                         writes=[("wgu", buf, 0)])
                self.dma("pool", wgu[:, buf, 1].rearrange("p j k f -> p (j k f)"), self.d_wu[which][g],
                         writes=[("wgu", buf, 1)])
            for jq in range(nj):
                self.dma("pool", wd[:, wb, jq, :], self.d_wd[which][2 * g0 + jq], writes=[("wd", wb, jq)],
                         reads=([("x", 0, 3)] if (which == 0 and q == 0) else ()))
            if q == 0:
                order = [(g, j, tt) for tt in range(4) for g in range(g0, g1) for j in range(2)]
            else:
                order = [(g, j, tt) for g in range(g0, g1) for j in range(2) for tt in range(4)]
            for g, j, tt in order:
                buf = g % 3
                if True:
                    jq = 2 * (g - g0) + j
                    if True:
                        par = gu_i % 2
                        gu_i += 1
                        bg, bu = 2 * par, 2 * par + 1
                        for k in range(8):
                            self.mm(self.psb(bg), wgu[:, buf, 0, j, k, :], self.hT[:, k, tt * 512:(tt + 1) * 512],
                                    k == 0, k == 7, reads=[("wgu", buf, 0), ("h", k, tt)], writes=[("ps", bg)])
                        for k in range(8):
                            self.mm(self.psb(bu), wgu[:, buf, 1, j, k, :], self.hT[:, k, tt * 512:(tt + 1) * 512],
                                    k == 0, k == 7, reads=[("wgu", buf, 1), ("h", k, tt)], writes=[("ps", bu)])
                        self.act(st[:, par, :], self.psb(bg), AF.Silu, reads=[("ps", bg)], writes=[("st", par)])
                        self.tt(aT[:, jq, tt * 512:(tt + 1) * 512], st[:, par, :], self.psb(bu), ALU.mult,
                                reads=[("st", par), ("ps", bu)], writes=[("a", jq, tt)])
            for tt in range(4):
                for c in range(8):
                    b = self.bank(4, 8)
                    for jq in range(nj):
                        self.mm(self.psb(b), wd[:, wb, jq, c * 128:(c + 1) * 128], aT[:, jq, tt * 512:(tt + 1) * 512],
                                jq == 0, jq == nj - 1, reads=[("wd", wb, jq), ("a", jq, tt)], writes=[("ps", b)])
                    xs = self.xT[:, c, tt * 512:(tt + 1) * 512]
                    self.stt(xs, self.psb(b), 0.5, xs, ALU.mult, ALU.add,
                             reads=[("ps", b), ("x", c, tt)], writes=[("x", c, tt)])
                if q == 3 and tail is not None:
                    tail(tt)

    def mixer(self):
        S = self.S
        OFF_Q = 0
        self.qaug = [self.carve(OFF_Q + 4096 * e, 4096, BF16) for e in range(2)]
        self.kaug = [self.carve(8192 + 4096 * e, 4096, BF16) for e in range(2)]
        self.V = self.carve(16384, 8192, BF16, "p (e s d) -> p e s d", e=2, s=16)
        self.LtE = [self.carve(24576, 16384, BF16, "p (a b) -> p a b", b=512),
                    self.carve(81920, 16384, BF16, "p (a b) -> p a b", b=512)]
        self.et = self.carve(40960, 12288, F32, "p (a b) -> p a b", a=2)
        self.At = self.carve(53248, 6144, BF16, "p (a b) -> p a b", a=2)
        self.wqkv = self.carve(102400, 6144, BF16, "p (n j k f) -> p n j k f", n=1, j=3, k=8)
        self.yT = [self.carve(65536, 16384, BF16, "p (a b) -> p a b", b=T),
                   self.carve(81920, 16384, BF16, "p (a b) -> p a b", b=T)]
        self.Pc = self.carve(98304, 4096, BF16)
        self.At2 = self.carve(59392, 6144, BF16, "p (a b) -> p a b", a=2)
        src0 = self.d_win[0:9:4].rearrange("c p x -> p c x")
        self.dma("pool", self.wqkv[:, 0].rearrange("p j k f -> p j (k f)"), src0, writes=[("wqkv", 0)])
        S.barrier()
        wf = self.wf[:].rearrange("p (k f) -> p k f", f=8)
        ef = self.carve(81920, 8192, F32)
        Pf = self.carve(90112, 8192, F32)
        r1 = ef
        Ps = self.carve(65536, 12288, BF16).rearrange("p (i t) -> p i t", i=3)
        self.dma("pool", self.wf[:], self.d_wf, writes=["wf"])
        S.add("dve", lambda v: v.tensor_scalar(out=self.negb[:], in0=self.bfor[:], scalar1=-1.0, scalar2=None, op0=ALU.mult),
              reads=["bfor"], writes=["negb"])
        for tt in range(4):
            b = self.bank(0, 4)
            for k in range(8):
                self.mm(self.psb(b, slice(0, 8)), wf[:, k, :], self.hT[:, k, tt * 512:(tt + 1) * 512], k == 0, k == 7,
                        reads=["wf", ("h", k, tt)], writes=[("ps", b)])
            self.act(ef[0:8, tt * 512:(tt + 1) * 512], self.psb(b, slice(0, 8)), AF.Exp, reads=[("ps", b), "negb"],
                     writes=["ef"], bias=self.negb[:], scale=-1.0)
        self.act(ef[0:8, :], ef[0:8, :], AF.Ln, reads=["ef"], writes=["ef"], bias=1.0)
        S.add("dve", lambda v: v.tensor_tensor_scan(out=Pf[0:8, :], data0=ef[0:8, :], data1=ef[0:8, :], initial=0.0,
                                                    op0=ALU.add, op1=ALU.bypass), reads=["ef"], writes=["Pf"])
        self.copy(Ps[0:8, 0, :], Pf[0:8, :], reads=["Pf"], writes=["Ps0"])
        self.tt(r1[0:8, :], Pf[0:8, :], Ps[0:8, 0, :], ALU.subtract, reads=["Pf", "Ps0", "ef"], writes=["r1"])
        self.copy(Ps[0:8, 1, :], r1[0:8, :], reads=["r1"], writes=["Ps1"])
        self.tt(r1[0:8, :], r1[0:8, :], Ps[0:8, 1, :], ALU.subtract, reads=["r1", "Ps1"], writes=["r1"])
        self.copy(Ps[0:8, 2, :], r1[0:8, :], reads=["r1"], writes=["Ps2"])
        for i3 in range(3):
            self.dma("sp", self.Pc[8 * i3:8 * i3 + 8, :], Ps[0:8, i3, :], reads=["Ps0", "Ps1", "Ps2"], writes=["Pc"])
        for e in range(2):
            A = 64 if e == 0 else 0
            self.dma("pool", self.kaug[e][A:A + 64, :], self.d_caug[0], writes=[("kaug", e)])
        for p in range(4):
            self.project(p, 0)
            self.emit_items(self.sb_pair_items(p))
        S.barrier()
        for e in range(2):
            A = 64 if e == 0 else 0
            self.dma("pool", self.qaug[e][A:A + 64, :], self.d_caug[1], writes=[("qaug", e)])
            self.dma("pool", self.kaug[e][A:A + 64, :], self.d_caug[2], writes=[("kaug", e)])
        S.add("pool", lambda g: g.memset(self.V[:, 0, :, 64:128], 1.0), writes=["V"])
        S.add("pool", lambda g: g.memset(self.V[:, 1, :, 0:64], 1.0), writes=["V"])
        for p in range(4):
            self.project(p, 1)
            if p == 0:
                wup_ = self.wup_tiles()
                self.dma("pool", wup_[0].rearrange("p k n -> p (k n)"), self.d_wupa, writes=["wupa"])
                self.dma("pool", wup_[1].rearrange("p k n -> p (k n)"), self.d_wupb, writes=["wupb"])
            fa, fb = self.fox_items(p, 0), self.fox_items(p, 1)
            zipped = []
            for ia, ib in zip(fa, fb):
                zipped += [ia, ib]
            self.emit_items(zipped)
        if self.dbg:
            for br in range(2):
                self.dma("sp", self.d_dbg[br], self.yT[br].rearrange("p a b -> p (a b)"),
                         reads=[("y", br, c, tt) for c in range(4) for tt in range(4)])
        S.barrier()
        self.out_stage()

    def project(self, p, br):
        buf = 0
        base = 12 * br
        src = self.d_win[base + p:base + p + 9:4].rearrange("c p x -> p c x")
        if not (p == 0 and br == 0):
            self.dma("pool", self.wqkv[:, buf].rearrange("p j k f -> p j (k f)"), src, writes=[("wqkv", buf)])
        for which, dst, scale in ((0, self.qaug, 0.125), (1, self.kaug, 1.0)):
            key = "qaug" if which == 0 else "kaug"
            for tt in range(4):
                if br == 0 and tt >= 2:
                    break
                b = self.bank(0, 4)
                for k in range(8):
                    self.mm(self.psb(b), self.wqkv[:, buf, which, k, :], self.hT[:, k, tt * 512:(tt + 1) * 512],
                            k == 0, k == 7, reads=[("wqkv", buf), ("h", k, tt)], writes=[("ps", b)])
                for e in range(2):
                    rows = slice(0, 64) if e == 0 else slice(64, 128)
                    self.act(dst[e][rows, tt * 512:(tt + 1) * 512], self.psb(b, rows), AF.Copy,
                             reads=[("ps", b)], writes=[(key, e)], scale=scale)
        for s4 in range(4):
            if br == 0:
                break
            b = self.bank(0, 4)
            for sb_ in range(4):
                blk = s4 * 4 + sb_
                for k in range(8):
                    self.mm(self.psb(b, c0=sb_ * 128, c1=(sb_ + 1) * 128), self.hT[:, k, blk * 128:(blk + 1) * 128],
                            self.wqkv[:, buf, 2, k, :], k == 0, k == 7,
                            reads=[("wqkv", buf), ("h", k, s4)], writes=[("ps", b)])
            pv = self.psb(b).rearrange("p (s d) -> p s d", d=128)
            if br == 0:
                self.copy(self.V[:, 0, s4 * 4:(s4 + 1) * 4, :], pv, reads=[("ps", b)], writes=["V"])
            else:
                self.copy(self.V[:, 0, s4 * 4:(s4 + 1) * 4, 0:64], pv[:, :, 0:64], reads=[("ps", b)], writes=["V"])
                self.copy(self.V[:, 1, s4 * 4:(s4 + 1) * 4, 64:128], pv[:, :, 64:128], reads=[("ps", b)], writes=["V"])
        if br == 1:
            for e in range(2):
                A = 64 if e == 0 else 0
                h = 2 * p + e
                self.dma("sp", self.qaug[e][A:A + 3, :], self.Pc[h:24:8, :], reads=["Pc"], writes=[("qaug", e)])
                self.dma("sp", self.kaug[e][A + 3:A + 6, :], self.Pc[h:24:8, :], reads=["Pc"], writes=[("kaug", e)])

    DG_OFF = {0: 0, 1: 512, 3: 896, 2: 1024}

    def batches(self, e, g, Lt):
        out = []
        kb = 0
        while kb < 4 * g:
            m = min(3, 4 * g - kb)
            segs = []
            for i in range(m):
                segs.append(dict(kb=kb + i, off=512 * i, w=512, qc=0, dg=False,
                                 lt=(Lt[:, kb + i, :] if Lt is not None else None)))
            lt_all = Lt[:, kb:kb + m, :].rearrange("p a b -> p (a b)") if Lt is not None else None
            out.append(dict(segs=segs, W=512 * m, lt_all=lt_all))
            kb += m
        ltf = Lt[:, 4 * g:4 * g + 4, :].rearrange("p a b -> p (a b)") if Lt is not None else None
        segs = []
        for j in range(4):
            off = self.DG_OFF[j]
            w = 512 - 128 * j
            segs.append(dict(kb=4 * g + j, off=off, w=w, qc=128 * j, dg=True,
                             lt=(ltf[:, off:off + w] if Lt is not None else None)))
        out.append(dict(segs=segs, W=1280, lt_all=(ltf[:, 0:1280] if Lt is not None else None)))
        return out

    def seg_ps(self, e, sg):
        c = 1536 * e + sg["off"]
        return self.ps[:, c:c + sg["w"]]

    def seg_key(self, e, sg):
        return ("ps", 3 * e + sg["off"] // 512)

    def attn_scores(self, e, g, bt, aug, mask):
        qa, ka = self.qaug[e], self.kaug[e]
        qrows = slice(0, 128) if aug else (slice(0, 64) if e == 0 else slice(64, 128))
        G0 = 512 * g
        msk = self.maskS if mask == 0 else self.maskC
        for sg in bt["segs"]:
            kb, dg = sg["kb"], sg["dg"]
            out = self.seg_ps(e, sg)
            key = self.seg_key(e, sg)
            tri = aug and mask == 0
            self.mm(out, ka[qrows, kb * 128:(kb + 1) * 128], qa[qrows, G0 + sg["qc"]:G0 + 512],
                    True, (not dg) and not tri, reads=[("kaug", e), ("qaug", e)], writes=[key])
            if tri:
                self.mm(out, self.triN, sg["lt"], False, not dg, reads=[("Lt", e, kb), "cm"], writes=[key])
            if dg:
                self.mm(out[:, 0:128], self.ident, msk, False, True, reads=["cm"], writes=[key])

    def head_oplist(self, items):
        real = self.S
        rec = Recorder()
        self.S = rec
        pend = []
        for first, pe1, act, pe2 in items:
            if first:
                for f in pend:
                    f()
                pend = []
            if pe1:
                pe1()
            for f in pend:
                f()
            pend = []
            if act:
                act()
            if pe2:
                pend.append(pe2)
        for f in pend:
            f()
        self.S = real
        return rec.ops

    def interleave(self, seqs, delay):
        lists = [self.head_oplist(it) for it in seqs]
        S = self.S
        ptr = [0] * len(lists)
        free = {e: 0.0 for e in ENGS}
        avail = {}
        lastrd = {}
        LAT = 0.12
        while True:
            best = None
            for h in range(len(lists)):
                if ptr[h] >= len(lists[h]):
                    continue
                eng, fn, rd, wr, dur = lists[h][ptr[h]]
                t = free[eng]
                for k in rd:
                    t = max(t, avail.get(k, 0.0) + LAT)
                for k in wr:
                    t = max(t, avail.get(k, 0.0) + LAT, lastrd.get(k, 0.0) + LAT)
                if best is None or t < best[0]:
                    best = (t, h)
            if best is None:
                break
            t, h = best
            eng, fn, rd, wr, dur = lists[h][ptr[h]]
            ptr[h] += 1
            S.add(eng, fn, reads=rd, writes=wr)
            end = t + dur
            free[eng] = end
            for k in wr:
                avail[k] = end
            for k in rd:
                lastrd[k] = max(lastrd.get(k, 0.0), end)

    def qk_unit(self, which, tt, b):
        dst = self.qaug if which == 0 else self.kaug
        key = "qd" if which == 0 else "kd"
        ts = slice(tt * 512, (tt + 1) * 512)
        for k in range(8):
            self.mm(self.psb(b), self.wqkv[:, 0, which, k, :], self.hT[:, k, ts], k == 0, k == 7,
                    reads=[("wqkv", 0), ("h", k, tt)], writes=[("ps", b)])
        for e in range(2):
            rows = slice(0, 64) if e == 0 else slice(64, 128)
            if which == 0:
                self.S.add("dve", lambda v, o=dst[e][rows, ts], i_=self.psb(b, rows):
                           v.tensor_scalar(out=o, in0=i_, scalar1=0.125, scalar2=None, op0=ALU.mult),
                           reads=[("ps", b)], writes=[(key, e, tt)], dur=0.7)
            else:
                self.copy(dst[e][rows, ts], self.psb(b, rows), reads=[("ps", b)], writes=[(key, e, tt)])

    def v_unit(self, s4, b):
        for sb_ in range(4):
            blk = s4 * 4 + sb_
            for k in range(8):
                self.mm(self.psb(b, c0=sb_ * 128, c1=(sb_ + 1) * 128), self.hT[:, k, blk * 128:(blk + 1) * 128],
                        self.wqkv[:, 0, 2, k, :], k == 0, k == 7,
                        reads=[("wqkv", 0), ("h", k, s4)], writes=[("ps", b)])
        pv = self.psb(b).rearrange("p (s d) -> p s d", d=128)
        self.copy(self.V[:, 0, s4 * 4:(s4 + 1) * 4, :], pv, reads=[("ps", b)], writes=[("V", s4)])

    def emit_items(self, items):
        pend = []
        for first, pe1, act, pe2 in items:
            if first:
                for f in pend:
                    f()
                pend = []
            if pe1:
                pe1()
            for f in pend:
                f()
            pend = []
            if act:
                act()
            if pe2:
                pend.append(pe2)
        for f in pend:
            f()

    def sb_pair_items(self, p):
        H = []
        for e in range(2):
            H.append(dict(
                e=e, qa=self.qaug[e], ka=self.kaug[e],
                qrows=slice(0, 64) if e == 0 else slice(64, 128),
                arows=slice(64, 128) if e == 0 else slice(0, 64),
                lorows=slice(96, 128) if e == 0 else slice(32, 64),
                Vh=self.V[:, 0, :, 64 * e:64 * e + 64],
                Lt=self.LtE[e], B=6 + e, zc=1536 * e))

        def ltkeys(e, g, sg):
            if sg["dg"]:
                return [("Lt", e, 4 * g + j) for j in range(4)]
            return [("Lt", e, sg["kb"])]

        def p1_items(g):
            G0 = 512 * g
            bts = [self.batches(e, g, H[e]["Lt"]) for e in range(2)]
            nb = len(bts[0])
            nk = 4 * g + 4
            out = []
            first_b = nb - 1 if g >= 1 else 0
            last_b = nb - 2 if g >= 1 else nb - 1
            for bi in range(nb):
                bt = [bts[0][bi], bts[1][bi]]
                W = bt[0]["W"]
                ns = len(bt[0]["segs"])
                kbs = [sg["kb"] for sg in bt[0]["segs"]]

                def pe1(bt=bt, ns=ns):
                    for si in range(ns):
                        for h in H:
                            e = h["e"]
                            sg = bt[e]["segs"][si]
                            kb = sg["kb"]
                            self.mm(self.seg_ps(e, sg), h["ka"][h["qrows"], kb * 128:(kb + 1) * 128],
                                    h["qa"][h["qrows"], G0 + sg["qc"]:G0 + 512], True, not sg["dg"],
                                    reads=[("kaug", e), ("qaug", e), ("kd", e, kb // 4), ("qd", e, g)],
                                    writes=[self.seg_key(e, sg)])
                        for h in H:
                            e = h["e"]
                            sg = bt[e]["segs"][si]
                            if sg["dg"]:
                                self.mm(self.seg_ps(e, sg)[:, 0:128], self.ident, self.maskS, False, True,
                                        reads=["cm"], writes=[self.seg_key(e, sg)])

                def a_exp(bt=bt, W=W):
                    banks = []
                    for e in range(2):
                        banks += sorted(set(self.seg_key(e, sg) for sg in bt[e]["segs"]))
                    src = self.ps[:, 0:3072].rearrange("p (e w) -> p e w", e=2)[:, :, 0:W]
                    self.act(self.et[:, :, 0:W], src, AF.Exp, reads=banks, writes=[("et", 0), ("et", 1)])

                def a_ln(bt=bt, W=W):
                    wk = []
                    for e in range(2):
                        for sg in bt[e]["segs"]:
                            wk += ltkeys(e, g, sg)
                    sg0 = bt[0]["segs"][0]
                    off = (4 * g if sg0["dg"] else sg0["kb"]) * 512
                    dst = bass.AP(self.R, 12288 + off, [[54272, 128], [28672, 2], [1, W]])
                    self.act(dst, self.et[:, :, 0:W], AF.Ln, reads=[("et", 0), ("et", 1), "Pc"],
                             writes=sorted(set(wk)), bias=1.0)

                def rmm(bt=bt, ns=ns, lastb=(bi == last_b), firstb=(bi == first_b)):
                    for si in range(ns):
                        for h in H:
                            e = h["e"]
                            sg = bt[e]["segs"][si]
                            kb = sg["kb"]
                            self.mm(self.psb(h["B"], h["arows"], sg["qc"], 512), self.lhsR[:, kb, :], sg["lt"],
                                    firstb and si == 0, lastb and si == ns - 1,
                                    reads=ltkeys(e, g, sg) + ["cm"], writes=[("ps", h["B"])])
                    if lastb:
                        for h in H:
                            e = h["e"]
                            qa = h["qa"]
                            self.copy(qa[h["arows"], G0:G0 + 512], self.psb(h["B"], h["arows"]), reads=[("ps", h["B"])],
                                      writes=[("qaug", e)])
                            self.tt(qa[h["lorows"], G0:G0 + 512], self.psb(h["B"], h["lorows"]), qa[h["lorows"], G0:G0 + 512],
                                    ALU.subtract, reads=[("ps", h["B"]), ("qaug", e)], writes=[("qaug", e)])

                its = [(False, pe1, a_exp, None), (False, None, a_ln, rmm)]
                if bi == 0:
                    base = 0 if g % 2 == 0 else 3

                    def units(g=g, base=base):
                        self.v_unit(g, base + 2)
                        if g in (1, 2):
                            self.qk_unit(0, g + 1, base + 1)
                            self.qk_unit(1, g + 1, base + 0)
                    its.insert(1, (False, units, None, None))
                out.append((kbs, its))
            if g >= 1:
                return [out[-1]], out[:-1]
            return [], out

        def p2_items(g):
            G0 = 512 * g
            bts = [self.batches(e, g, H[e]["Lt"]) for e in range(2)]
            nb = len(bts[0])
            nk = 4 * g + 4
            out = []
            for bi in range(nb):
                bt = [bts[0][bi], bts[1][bi]]
                W = bt[0]["W"]
                ns = len(bt[0]["segs"])
                kbs = [sg["kb"] for sg in bt[0]["segs"]]

                def pe1(bt=bt):
                    for e in range(2):
                        h = H[e]
                        for sg in bt[e]["segs"]:
                            kb, dg = sg["kb"], sg["dg"]
                            out_ = self.seg_ps(e, sg)
                            key = self.seg_key(e, sg)
                            self.mm(out_, h["ka"][:, kb * 128:(kb + 1) * 128], h["qa"][:, G0 + sg["qc"]:G0 + 512],
                                    True, False, reads=[("kaug", e), ("qaug", e), ("kd", e, kb // 4), ("qd", e, g)],
                                    writes=[key])
                            self.mm(out_, self.triN, sg["lt"], False, not dg, reads=ltkeys(e, g, sg) + ["cm"], writes=[key])
                            if dg:
                                self.mm(out_[:, 0:128], self.ident, self.maskS, False, True, reads=["cm"], writes=[key])

                ab = self.p2cnt % 2
                self.p2cnt += 1
                AtB = self.At if ab == 0 else self.At2

                def a_exp(bt=bt, W=W, AtB=AtB, ab=ab):
                    for h in H:
                        e = h["e"]
                        banks = sorted(set(self.seg_key(e, sg) for sg in bt[e]["segs"]))
                        self.act(AtB[:, e, 0:W], self.ps[:, h["zc"]:h["zc"] + W], AF.Exp, reads=banks, writes=[("At", e, ab)])

                def pv(bt=bt, ns=ns, lastb=(bi == nb - 1), AtB=AtB, ab=ab):
                    for si in range(ns):
                        for h in H:
                            e = h["e"]
                            sg = bt[e]["segs"][si]
                            kb = sg["kb"]
                            self.mm(self.psb(h["B"], h["qrows"], sg["qc"], 512), h["Vh"][:, kb, :],
                                    AtB[:, e, sg["off"]:sg["off"] + sg["w"]], kb == 0, kb == nk - 1,
                                    reads=[("At", e, ab), ("V", kb // 4)], writes=[("ps", h["B"])])
                    if lastb:
                        for h in H:
                            e = h["e"]
                            self.copy(self.yT[0][h["qrows"], p, G0:G0 + 512], self.psb(h["B"], h["qrows"]),
                                      reads=[("ps", h["B"]), "Pc"], writes=[("y", 0, p, g)])

                out.append((kbs, [(bi == 0, pe1, a_exp, pv)]))
            return out

        items = []
        p1parts = [p1_items(g) for g in range(4)]
        for kbs, its in p1parts[0][1]:
            items += its
        for kbs, its in p1parts[1][0]:
            items += its
        for g in range(4):
            p2 = p2_items(g)
            p1 = p1parts[g + 1][1] if g < 3 else []
            consumed = set()
            j = 0
            for kbs, its in p2:
                items += its
                consumed.update(kbs)
                while j < len(p1) and set(p1[j][0]) <= consumed:
                    items += p1[j][1]
                    j += 1
            while j < len(p1):
                items += p1[j][1]
                j += 1
            if g + 2 <= 3:
                for kbs, its in p1parts[g + 2][0]:
                    items += its
        return items

    def sb_items(self, p, e):
        qa = self.qaug[e]
        qrows = slice(0, 64) if e == 0 else slice(64, 128)
        arows = slice(64, 128) if e == 0 else slice(0, 64)
        lorows = slice(96, 128) if e == 0 else slice(32, 64)
        Vh = self.V[:, 0, :, 64 * e:64 * e + 64]
        Lt = self.LtE[e]
        RB = YB = 6 + e
        zc = 1536 * e
        items = []
        for g in range(4):
            G0 = 512 * g
            bts = self.batches(e, g, Lt)
            nb = len(bts)
            nk = 4 * g + 4
            for bi in range(nb):
                bt = bts[bi]
                W = bt["W"]
                banks = sorted(set(self.seg_key(e, sg) for sg in bt["segs"]))
                ltkeys = [("Lt", e, sg["kb"]) for sg in bt["segs"]]

                def pe1(bt=bt, g=g):
                    self.attn_scores(e, g, bt, False, 0)

                def a_exp(W=W, banks=banks):
                    self.act(self.et[:, e, 0:W], self.ps[:, zc:zc + W], AF.Exp, reads=banks, writes=[("et", e)])

                def a_ln(bt=bt, W=W, ltkeys=ltkeys):
                    self.act(bt["lt_all"], self.et[:, e, 0:W], AF.Ln,
                             reads=[("et", e), "Pc"], writes=ltkeys, bias=1.0)

                def rmm(bt=bt, nk=nk, lastb=(bi == nb - 1), G0=G0):
                    for sg in bt["segs"]:
                        kb = sg["kb"]
                        self.mm(self.psb(RB, arows, sg["qc"], 512), self.lhsR[:, kb, :], sg["lt"], kb == 0, kb == nk - 1,
                                reads=[("Lt", e, kb), "cm"], writes=[("ps", 6 + e)])
                    if lastb:
                        self.copy(qa[arows, G0:G0 + 512], self.psb(RB, arows), reads=[("ps", 6 + e)], writes=[("qaug", e)])
                        self.tt(qa[lorows, G0:G0 + 512], self.psb(RB, lorows), qa[lorows, G0:G0 + 512], ALU.subtract,
                                reads=[("ps", 6 + e), ("qaug", e)], writes=[("qaug", e)])

                items.append((bi == 0, pe1, a_exp, None))
                items.append((False, None, a_ln, rmm))
            for bi in range(nb):
                bt = bts[bi]
                W = bt["W"]
                banks = sorted(set(self.seg_key(e, sg) for sg in bt["segs"]))

                def pe1(bt=bt, g=g):
                    self.attn_scores(e, g, bt, True, 0)

                def a_exp(W=W, banks=banks):
                    self.act(self.At[:, e, 0:W], self.ps[:, zc:zc + W], AF.Exp, reads=banks, writes=[("At", e)])

                def pv(bt=bt, nk=nk, lastb=(bi == nb - 1), G0=G0, g=g):
                    for sg in bt["segs"]:
                        kb = sg["kb"]
                        self.mm(self.psb(YB, qrows, sg["qc"], 512), Vh[:, kb, :], self.At[:, e, sg["off"]:sg["off"] + sg["w"]],
                                kb == 0, kb == nk - 1, reads=[("At", e), "V"], writes=[("ps", 6 + e)])
                    if lastb:
                        self.copy(self.yT[0][qrows, p, G0:G0 + 512], self.psb(YB, qrows), reads=[("ps", 6 + e), "Pc"],
                                  writes=[("y", 0, p, g)])

                items.append((bi == 0, pe1, a_exp, pv))
        return items

    def fox_items(self, p, e):
        qrows = slice(0, 64) if e == 0 else slice(64, 128)
        arows = slice(64, 128) if e == 0 else slice(0, 64)
        Ve = self.V[:, e]
        rd = self.et
        YB = 6 + e
        zc = 1536 * e
        items = []
        for g in range(4):
            G0 = 512 * g
            bts = self.batches(e, g, None)
            nb = len(bts)
            nk = 4 * g + 4
            for bi in range(nb):
                bt = bts[bi]
                W = bt["W"]
                banks = sorted(set(self.seg_key(e, sg) for sg in bt["segs"]))

                def pe1(bt=bt, g=g):
                    self.attn_scores(e, g, bt, True, 1)

                def a_exp(W=W, banks=banks):
                    self.act(self.At[:, e, 0:W], self.ps[:, zc:zc + W], AF.Exp, reads=banks, writes=[("At", e)])

                def pv(bt=bt, nk=nk, lastb=(bi == nb - 1), G0=G0, g=g):
                    for sg in bt["segs"]:
                        kb = sg["kb"]
                        self.mm(self.psb(YB, slice(0, 128), sg["qc"], 512), Ve[:, kb, :],
                                self.At[:, e, sg["off"]:sg["off"] + sg["w"]], kb == 0, kb == nk - 1,
                                reads=[("At", e), "V"], writes=[("ps", 6 + e)])
                    if lastb:
                        ycp = rd[:, e, 512:1024]
                        self.copy(ycp, self.psb(YB), reads=[("ps", 6 + e)], writes=[("ycp", e)])
                        self.S.add("dve", lambda v, o=rd[qrows, e, 0:512], i_=ycp[arows, :]: v.reciprocal(out=o, in_=i_),
                                   reads=[("ycp", e)], writes=[("rd", e)], dur=3.4)
                        self.tt(self.yT[1][qrows, p, G0:G0 + 512], ycp[qrows, :], rd[qrows, e, 0:512], ALU.mult,
                                reads=[("ycp", e), ("rd", e)], writes=[("y", 1, p, g)])

                items.append((bi == 0, pe1, a_exp, pv))
        return items

    def wup_tiles(self):
        return [self.carve(24576, 8192, BF16, "p (k n) -> p k n", k=4), self.carve(32768, 8192, BF16, "p (k n) -> p k n", k=4)]

    def out_stage(self):
        S = self.S
        def mix(c):
            return self.carve(c * 4096, 4096, BF16) if c < 6 else self.carve(40960 + (c - 6) * 4096, 4096, BF16)
        wup = self.wup_tiles()
        sab = self.carve(49152, 4096, F32, "p (a b) -> p a b", b=512)
        wgt = self.carve(53248, 8192, BF16, "p (n g k f) -> p n g k f", n=2, g=2, k=8)
        wo = self.carve(98304, 4096, BF16, "p (n k f) -> p n k f", n=2, k=8)
        for c in range(8):
            buf = c % 2
            src = self.d_wgate[c:c + 9:8].rearrange("c p x -> p c x")
            self.dma("pool", wgt[:, buf].rearrange("p g k f -> p g (k f)"), src, writes=[("wgt", buf)])
            for tt in range(4):
                ts = slice(tt * 512, (tt + 1) * 512)
                bu = [self.bank(0, 8) for _ in range(4)]
                for br in range(2):
                    for k in range(4):
                        self.mm(self.psb(bu[br]), wup[br][:, k, c * 128:(c + 1) * 128], self.yT[br][:, k, ts], k == 0, k == 3,
                                reads=["wupa" if br == 0 else "wupb", ("y", br, k, tt)], writes=[("ps", bu[br])])
                for br in range(2):
                    for k in range(8):
                        self.mm(self.psb(bu[2 + br]), wgt[:, buf, br, k, :], self.hT[:, k, ts], k == 0, k == 7,
                                reads=[("wgt", buf), ("h", k, tt)], writes=[("ps", bu[2 + br])])
                for br in range(2):
                    self.act(sab[:, br, :], self.psb(bu[2 + br]), AF.Sigmoid, reads=[("ps", bu[2 + br]), "bgate"],
                             writes=[("sab", br)], bias=self.bgate[:, 8 * br + c:8 * br + c + 1])
                    self.tt(sab[:, br, :], sab[:, br, :], self.psb(bu[br]), ALU.mult,
                            reads=[("sab", br), ("ps", bu[br])], writes=[("sab", br)])
                self.tt(mix(c)[:, ts], sab[:, 0, :], sab[:, 1, :], ALU.add,
                        reads=[("sab", 0), ("sab", 1)], writes=[("mix", c, tt)])
        for c2 in range(8):
            buf = c2 % 2
            self.dma("pool", wo[:, buf].rearrange("p k f -> p (k f)"), self.d_wout[c2], writes=[("wo", buf)])
            for tt in range(4):
                ts = slice(tt * 512, (tt + 1) * 512)
                b = self.bank(0, 8)
                for k in range(8):
                    self.mm(self.psb(b), wo[:, buf, k, :], mix(k)[:, ts], k == 0, k == 7,
                            reads=[("wo", buf), ("mix", k, tt)], writes=[("ps", b)])
                xs = self.xT[:, c2, ts]
                self.tt(xs, self.psb(b), xs, ALU.add, reads=[("ps", b), ("x", c2, tt)], writes=[("x", c2, tt)])

def _consts():
    cm = np.zeros((128, 1664), np.float32)
    i = np.arange(128)
    cm[:, 0:128] = (i[:, None] == i[None, :])
    cm[:, 128:256] = np.where(i[:, None] >= i[None, :], NEG, 0.0)
    cm[:, 256:384] = np.where(i[:, None] > i[None, :], NEG, 0.0)
    cm[:, 384:512] = np.where(i[:, None] >= i[None, :], -1.0, 0.0)
    cm[:, 512:640] = 1.0 / 1024.0
    lr = np.zeros((16, 64), np.float32)
    for kb in range(16):
        for r in range(64):
            if (r < 16 and r < kb) or (32 <= r < 48 and (r - 32) < kb):
                lr[kb, r] = -1.0
    cm[:, 640:1664] = lr.reshape(1, 1024)
    caug = np.zeros((3, 64, 2048), np.float32)
    for kb in range(16):
        caug[0, kb, kb * 128:(kb + 1) * 128] = 1.0
        caug[0, 32 + kb, kb * 128:(kb + 1) * 128] = 1.0
    caug[1, 3:6, :] = 1.0
    caug[2, 0:3, :] = -1.0
    return cm, caug


def _chunked(w, kdim):
    K, N = w.shape
    a = w.reshape(K // 128, 128, N // 128, 128).transpose(2, 1, 0, 3)
    return np.ascontiguousarray(a).reshape(N // 128, 128, K)


def prepare_shared(inp):
    f = lambda a: np.asarray(a, dtype=np.float32)
    sh = {}
    ffn_w = ((1, inp["w_ffn1_gate"], inp["w_ffn1_up"], inp["w_ffn1_down"]),
             (2, inp["w_ffn2_gate"], inp["w_ffn2_up"], inp["w_ffn2_down"]))
    for i, w_g, w_u, w_d in ffn_w:
        wg = _chunked(f(w_g)[0], 1024)
        wu = _chunked(f(w_u)[0], 1024)
        sh["wg%d" % i] = np.ascontiguousarray(wg.reshape(11, 2, 128, 1024).transpose(0, 2, 1, 3)).reshape(11, 128, 2048)
        sh["wu%d" % i] = np.ascontiguousarray(wu.reshape(11, 2, 128, 1024).transpose(0, 2, 1, 3)).reshape(11, 128, 2048)
        sh["wd%d" % i] = np.ascontiguousarray(f(w_d)[0].reshape(22, 128, 1024))
    win = f(inp["w_in"])[0]
    sh["win"] = _chunked(win[:, :3072], 1024)
    sh["wf"] = np.ascontiguousarray(win[:, 3072:3080].reshape(8, 128, 8).transpose(1, 0, 2)).reshape(128, 64)
    sh["wgate"] = _chunked(f(inp["w_gate"])[0], 1024)
    sh["wupa"] = np.ascontiguousarray(f(inp["w_up_a"])[0].reshape(4, 128, 1024).transpose(1, 0, 2)).reshape(128, 4096)
    sh["wupb"] = np.ascontiguousarray(f(inp["w_up_b"])[0].reshape(4, 128, 1024).transpose(1, 0, 2)).reshape(128, 4096)
    sh["wout"] = _chunked(f(inp["w_out"])[0], 1024)
    gv = np.stack([f(inp["norm_ffn1"])[0], f(inp["norm_mix"])[0], f(inp["norm_ffn2"])[0], f(inp["norm_final"])], 0)
    sh["gvec"] = np.ascontiguousarray(gv.reshape(4, 8, 128).transpose(2, 0, 1)).reshape(128, 32)
    sh["bgate"] = np.ascontiguousarray(f(inp["b_gate"])[0].reshape(16, 128).T)
    sh["bfor"] = np.ascontiguousarray(f(inp["b_forget"])[0].reshape(8, 1))
    cm, caug = _consts()
    sh["cmat"] = cm
    sh["caug"] = caug
    return sh


_NC_CACHE = {}


def kernel(**inputs):
    x = np.asarray(inputs["x"], dtype=np.float32)
    B = x.shape[0]
    sh = prepare_shared(inputs)
    if "nc" not in _NC_CACHE:
        _NC_CACHE["nc"] = Builder(stage=3).build()
    nc = _NC_CACHE["nc"]
    in_maps = []
    for b in range(B):
        m = dict(sh)
        m["xT"] = np.ascontiguousarray(x[b].T)
        in_maps.append(m)
    res = run_bass_kernel_spmd(nc, in_maps, core_ids=list(range(B)))
    out = np.stack([np.asarray(res.results[b]["outT"]).T for b in range(B)], 0)
    return np.ascontiguousarray(out.astype(np.float32))
```

```python
import numpy as np
from contextlib import ExitStack
import concourse.bass as bass
import concourse.mybir as mybir
from concourse.bass_utils import run_bass_kernel_spmd

F32 = mybir.dt.float32
BF16 = mybir.dt.bfloat16
AF = mybir.ActivationFunctionType
ALU = mybir.AluOpType

T = 2048
D = 1024
DFF = 2816
NEG = -30000.0
ENGS = ("pe", "act", "dve", "pool", "sp")


class Op:
    __slots__ = ("eng", "idx", "fn", "deps", "dma", "sig", "needed", "name", "_slot_prev", "_barriered")

    def __init__(self, eng, idx, fn, dma, name):
        self.eng = eng
        self.idx = idx
        self.fn = fn
        self.dma = dma
        self.deps = []
        self.sig = None
        self.needed = False
        self.name = name
        self._slot_prev = 0
        self._barriered = False


class Sched:
    def __init__(self, nc, same_engine_sync=True, dma_slots=8):
        self.nc = nc
        self.ops = {e: [] for e in ENGS}
        self.last_w = {}
        self.readers = {}
        self.same_engine_sync = same_engine_sync
        self.dma_slots = dma_slots

    def add(self, eng, fn, reads=(), writes=(), dma=False, name="", extra_deps=(), dur=0.0):
        op = Op(eng, len(self.ops[eng]), fn, dma, name)
        deps = []

        def want(o, kind):
            if o is None or o is op:
                return
            if o.eng == eng and not o.dma and not dma:
                if eng == "pe":
                    return
                if kind != "raw" or not self.same_engine_sync:
                    return
            deps.append(o)

        for k in reads:
            want(self.last_w.get(k), "raw")
        for k in writes:
            want(self.last_w.get(k), "waw")
            for r in self.readers.get(k, ()):
                want(r, "war")
        for o in extra_deps:
            want(o, "raw")
        best = {}
        out = []
        for o in deps:
            if o.dma:
                if o not in out:
                    out.append(o)
            else:
                b = best.get(o.eng)
                if b is None or o.idx > b.idx:
                    best[o.eng] = o
        out.extend(best.values())
        op.deps = out
        for o in out:
            o.needed = True
        for k in writes:
            self.last_w[k] = op
            self.readers[k] = []
        for k in reads:
            lst = self.readers.setdefault(k, [])
            if not dma:
                lst[:] = [r for r in lst if r.dma or r.eng != eng]
            lst.append(op)
        self.ops[eng].append(op)
        return op

    def barrier(self):
        lasts = []
        for e in ENGS:
            ol = self.ops[e]
            if ol:
                for o in reversed(ol):
                    if not o.dma and o.fn is not None:
                        lasts.append(o)
                        break
                lasts.extend(o for o in ol if o.dma and not o._barriered)
        for o in lasts:
            if o.dma:
                o._barriered = True
        for e in ENGS:
            self.add(e, None, extra_deps=list(lasts), name="barrier")

    def emit(self, sems):
        for e in ENGS:
            cnt = 0
            slot_uses = [0] * self.dma_slots
            nd = 0
            for o in self.ops[e]:
                if o.dma:
                    s = nd % self.dma_slots
                    nd += 1
                    slot_uses[s] += 1
                    o.sig = (sems["dma_%s_%d" % (e, s)], 16 * slot_uses[s], 16)
                    o._slot_prev = 16 * (slot_uses[s] - 1)
                elif o.needed and o.fn is not None:
                    cnt += 1
                    o.sig = (sems[e], cnt, 1)
                elif o.needed and o.fn is None:
                    raise RuntimeError("dependency on pure-wait op")
        engobj = {"pe": "tensor", "act": "scalar", "dve": "vector", "pool": "gpsimd", "sp": "sync"}

        def run(e, eng):
            waited = {}
            for o in self.ops[e]:
                ws = []
                for d in o.deps:
                    sem, val, _ = d.sig
                    ws.append((sem, val))
                if o.dma and o._slot_prev > 0:
                    ws.append((o.sig[0], o._slot_prev))
                ws.sort(key=lambda x: -x[1])
                for sem, val in ws:
                    key = id(sem)
                    if waited.get(key, 0) >= val:
                        continue
                    waited[key] = val
                    eng.wait_ge(sem, val)
                if o.fn is not None:
                    ins = o.fn(eng)
                    if o.sig is not None:
                        ins.then_inc(o.sig[0], o.sig[2])

        with self.nc.Block() as block:
            for e in ENGS:
                if not self.ops[e]:
                    continue
                deco = getattr(block, engobj[e])

                def make(e=e):
                    def body(eng):
                        run(e, eng)
                    return body

                deco(make())


def _fsize(ap):
    n = 1
    for d in ap.shape[1:]:
        n *= d
    return n


class Recorder:
    def __init__(self):
        self.ops = []

    def add(self, eng, fn, reads=(), writes=(), dma=False, name="", extra_deps=(), dur=0.0):
        assert not dma
        self.ops.append((eng, fn, tuple(reads), tuple(writes), dur))


class Builder:
    def __init__(self, stage=3, dbg=False):
        self.stage = stage
        self.dbg = dbg
        self.nc = bass.Bass("TRN2", target_bir_lowering=False)
        self.bank_rr = 0

    def dram_in(self, name, shape, dt=F32):
        return self.nc.dram_tensor(name, list(shape), dt, kind="ExternalInput").ap()

    def mm(self, out, lhsT, rhs, start, stop, reads, writes):
        self.S.add("pe", lambda t, o=out, l=lhsT, r=rhs, a=start, b=stop: t.matmul(o, lhsT=l, rhs=r, start=a, stop=b),
                   reads=reads, writes=writes, dur=0.03 + out.shape[-1] * 0.00043)

    def act(self, out, in_, func, reads, writes, bias=None, scale=None):
        kw = {}
        if bias is not None:
            kw["bias"] = bias
        if scale is not None:
            kw["scale"] = scale
        self.S.add("act", lambda s, o=out, i=in_, f=func, kw=kw: s.activation(out=o, in_=i, func=f, **kw),
                   reads=reads, writes=writes, dur=(_fsize(out) + 335) / 1200.0)

    def tt(self, out, in0, in1, op, reads, writes, eng="dve"):
        self.S.add(eng, lambda v, o=out, a=in0, b=in1, op=op: v.tensor_tensor(out=o, in0=a, in1=b, op=op),
                   reads=reads, writes=writes, dur=0.15 + _fsize(out) / 960.0)

    def stt(self, out, in0, scalar, in1, op0, op1, reads, writes):
        self.S.add("dve", lambda v, o=out, a=in0, s=scalar, b=in1, p0=op0, p1=op1:
                   v.scalar_tensor_tensor(out=o, in0=a, scalar=s, in1=b, op0=p0, op1=p1), reads=reads, writes=writes)

    def copy(self, out, in_, reads, writes, eng="dve"):
        self.S.add(eng, lambda v, o=out, i=in_: v.tensor_copy(out=o, in_=i), reads=reads, writes=writes,
                   dur=0.15 + _fsize(out) / 960.0)

    def dma(self, eng, out, in_, reads=(), writes=()):
        return self.S.add(eng, lambda g, o=out, i=in_: g.dma_start(out=o, in_=i), reads=reads, writes=writes, dma=True)

    def bank(self, lo=0, hi=4):
        b = lo + self.bank_rr % (hi - lo)
        self.bank_rr += 1
        return b

    def psb(self, b, rows=slice(0, 128), c0=0, c1=512):
        return self.ps[rows, b * 512 + c0:b * 512 + c1]

    def carve(self, off, nbytes, dt, pattern=None, **kw):
        a = self.R[:, off // 2:(off + nbytes) // 2]
        if dt == F32:
            a = a.bitcast(F32)
        if pattern:
            a = a.rearrange(pattern, **kw)
        return a

    def build(self):
        nc = self.nc
        din = self.dram_in
        self.d_xT = din("xT", [D, T])
        self.d_wg = [din("wg1", [11, 128, 2048]), din("wg2", [11, 128, 2048])]
        self.d_wu = [din("wu1", [11, 128, 2048]), din("wu2", [11, 128, 2048])]
        self.d_wd = [din("wd1", [22, 128, 1024]), din("wd2", [22, 128, 1024])]
        self.d_win = din("win", [24, 128, 1024])
        self.d_wf = din("wf", [128, 64])
        self.d_wgate = din("wgate", [16, 128, 1024])
        self.d_wupa = din("wupa", [128, 4096])
        self.d_wupb = din("wupb", [128, 4096])
        self.d_wout = din("wout", [8, 128, 1024])
        self.d_gvec = din("gvec", [128, 32])
        self.d_bgate = din("bgate", [128, 16])
        self.d_bfor = din("bfor", [8, 1])
        self.d_cmat = din("cmat", [128, 1664])
        self.d_caug = din("caug", [3, 64, 2048])
        self.d_out = nc.dram_tensor("outT", [D, T], F32, kind="ExternalOutput").ap()
        if self.dbg:
            self.d_dbg = nc.dram_tensor("dbgy", [2, 128, 4 * T], BF16, kind="ExternalOutput").ap()

        with ExitStack() as es:
            def sbuf(name, shape, dt):
                return es.enter_context(nc.sbuf_tensor(name, shape, dt))
            self.xT = sbuf("xT_sb", [128, 8, T], F32)
            self.hT = sbuf("hT_sb", [128, 8, T], BF16)
            self.R = sbuf("R_sb", [128, 54272], BF16)
            self.cm = sbuf("cm_sb", [128, 1664], BF16)
            self.gvec = sbuf("gvec_sb", [128, 32], F32)
            self.bgate = sbuf("bgate_sb", [128, 16], F32)
            self.bfor = sbuf("bfor_sb", [8, 1], F32)
            self.wf = sbuf("wf_sb", [128, 64], BF16)
            self.negb = sbuf("negb_sb", [8, 1], F32)
            self.ps = es.enter_context(nc.psum_tensor("ps", [128, 4096], F32))
            sems = {}
            for e in ENGS:
                sems[e] = es.enter_context(nc.semaphore("c_" + e))
                for i in range(8):
                    sems["dma_%s_%d" % (e, i)] = es.enter_context(nc.semaphore("d_%s_%d" % (e, i)))
            self.S = Sched(nc)
            self.sq_i = 0
            self.p2cnt = 0
            self.ident = self.cm[:, 0:128]
            self.maskS = self.cm[:, 128:256]
            self.maskC = self.cm[:, 256:384]
            self.triN = self.cm[:, 384:512]
            self.meanm = self.cm[:, 512:640]
            self.lhsR = self.cm[:, 640:1664].rearrange("p (a b) -> p a b", b=64)

            self.program()
            self.S.emit(sems)
        return nc

    def program(self):
        S = self.S
        xsrc = self.d_xT.rearrange("(c p) t -> p c t", p=128)
        for tt in range(4):
            self.dma("sp" if tt % 2 == 0 else "act", self.xT[:, :, tt * 512:(tt + 1) * 512], xsrc[:, :, tt * 512:(tt + 1) * 512],
                     writes=[("x", c, tt) for c in range(8)])
        self.dma("pool", self.cm[:], self.d_cmat, writes=["cm"])
        self.dma("sp", self.gvec[:], self.d_gvec, writes=["gvec"])
        self.dma("sp", self.bgate[:], self.d_bgate, writes=["bgate"])
        self.dma("sp", self.bfor[:], self.d_bfor, writes=["bfor"])

        full = self.stage >= 3
        self.ffn(0, tail=(lambda tt: self.norm_tt(tt, 1, "h")) if self.stage >= 2 else None)
        if self.stage >= 2:
            self.mixer()
        if full:
            S.barrier()
            self.ffn(1, tail=lambda tt: self.norm_tt(tt, 3, "final"))
        else:
            S.barrier()
            dst = self.d_out.rearrange("(c p) t -> p c t", p=128)
            for tt in range(4):
                ts = slice(tt * 512, (tt + 1) * 512)
                self.dma("sp", dst[:, :, ts], self.xT[:, :, ts], reads=[("x", c, tt) for c in range(8)])
        alld = [o for o in S.ops["sp"] if o.dma]
        S.add("sp", None, extra_deps=alld, name="final")

    def norm_scratch(self, off=77824):
        sq = self.carve(off, 2048, BF16, "p (a b) -> p a b", b=512)
        lnv = self.carve(off + 2048, 2048, F32)
        rstd = self.carve(off + 4096, 8192, F32, "p (a b) -> p a b", b=512)
        return sq, lnv, rstd

    def norm_tt(self, tt, gidx, mode):
        sq, lnv, rstd = self.norm_scratch()
        ts = slice(tt * 512, (tt + 1) * 512)
        b = self.bank(4, 8)
        for c in range(8):
            i = self.sq_i
            self.sq_i += 1
            self.act(sq[:, i % 2, :], self.xT[:, c, ts], AF.Square, reads=[("x", c, tt)], writes=[("sq", i % 2)])
            self.mm(self.psb(b), self.meanm, sq[:, i % 2, :], c == 0, c == 7,
                    reads=[("sq", i % 2), "cm"], writes=[("ps", b)])
        self.act(lnv, self.psb(b), AF.Ln, reads=[("ps", b)], writes=["lnv"], bias=1e-6)
        self.act(rstd[:, tt, :], lnv, AF.Exp, reads=["lnv"], writes=[("rstd", tt)], scale=-0.5)
        for c in range(8):
            if mode == "h":
                self.stt(self.hT[:, c, ts], self.xT[:, c, ts], self.gvec[:, gidx * 8 + c:gidx * 8 + c + 1], rstd[:, tt, :],
                         ALU.mult, ALU.mult, reads=[("x", c, tt), ("rstd", tt), "gvec"], writes=[("h", c, tt)])
            else:
                xs = self.xT[:, c, ts]
                self.stt(xs, xs, self.gvec[:, gidx * 8 + c:gidx * 8 + c + 1], rstd[:, tt, :], ALU.mult, ALU.mult,
                         reads=[("x", c, tt), ("rstd", tt), "gvec"], writes=[("x", c, tt)])
        if mode == "final":
            dst = self.d_out.rearrange("(c p) t -> p c t", p=128)
            self.dma("sp", dst[:, :, ts], self.xT[:, :, ts], reads=[("x", c, tt) for c in range(8)])

    def norm_to_h(self, gidx, off=77824):
        for tt in range(4):
            self.norm_tt(tt, gidx, "h")

    def ffn(self, which, tail=None, skip_norm=False):
        S = self.S
        aT = self.carve(0, 24576, BF16, "p (a b) -> p a b", b=T)
        wgu = self.carve(24576, 24576, BF16, "p (n g j k f) -> p n g j k f", n=3, g=2, j=2, k=8)
        wd = self.carve(49152, 24576, BF16, "p (n j d) -> p n j d", n=2, j=6)
        st = self.carve(73728, 4096, F32, "p (a b) -> p a b", b=512)
        if not skip_norm:
            self.norm_to_h(0 if which == 0 else 2)
        quarters = [(0, 3), (3, 6), (6, 9), (9, 11)]
        gu_i = 0
        for q, (g0, g1) in enumerate(quarters):
            nj = 2 * (g1 - g0)
            wb = q % 2
            for g in range(g0, g1):
                buf = g % 3
                self.dma("pool", wgu[:, buf, 0].rearrange("p j k f -> p (j k f)"), self.d_wg[which][g],
                         writes=[("wgu", buf, 0)])
                self.dma("pool", wgu[:, buf, 1].rearrange("p j k f -> p (j k f)"), self.d_wu[which][g],
                         writes=[("wgu", buf, 1)])
            for jq in range(nj):
                self.dma("pool", wd[:, wb, jq, :], self.d_wd[which][2 * g0 + jq], writes=[("wd", wb, jq)],
                         reads=([("x", 0, 3)] if (which == 0 and q == 0) else ()))
            if q == 0:
                order = [(g, j, tt) for tt in range(4) for g in range(g0, g1) for j in range(2)]
            else:
                order = [(g, j, tt) for g in range(g0, g1) for j in range(2) for tt in range(4)]
            for g, j, tt in order:
                buf = g % 3
                if True:
                    jq = 2 * (g - g0) + j
                    if True:
                        par = gu_i % 2
                        gu_i += 1
                        bg, bu = 2 * par, 2 * par + 1
                        for k in range(8):
                            self.mm(self.psb(bg), wgu[:, buf, 0, j, k, :], self.hT[:, k, tt * 512:(tt + 1) * 512],
                                    k == 0, k == 7, reads=[("wgu", buf, 0), ("h", k, tt)], writes=[("ps", bg)])
                        for k in range(8):
                            self.mm(self.psb(bu), wgu[:, buf, 1, j, k, :], self.hT[:, k, tt * 512:(tt + 1) * 512],
                                    k == 0, k == 7, reads=[("wgu", buf, 1), ("h", k, tt)], writes=[("ps", bu)])
                        self.act(st[:, par, :], self.psb(bg), AF.Silu, reads=[("ps", bg)], writes=[("st", par)])
                        self.tt(aT[:, jq, tt * 512:(tt + 1) * 512], st[:, par, :], self.psb(bu), ALU.mult,
                                reads=[("st", par), ("ps", bu)], writes=[("a", jq, tt)])
            for tt in range(4):
                for c in range(8):
                    b = self.bank(4, 8)
                    for jq in range(nj):
                        self.mm(self.psb(b), wd[:, wb, jq, c * 128:(c + 1) * 128], aT[:, jq, tt * 512:(tt + 1) * 512],
                                jq == 0, jq == nj - 1, reads=[("wd", wb, jq), ("a", jq, tt)], writes=[("ps", b)])
                    xs = self.xT[:, c, tt * 512:(tt + 1) * 512]
                    self.stt(xs, self.psb(b), 0.5, xs, ALU.mult, ALU.add,
                             reads=[("ps", b), ("x", c, tt)], writes=[("x", c, tt)])
                if q == 3 and tail is not None:
                    tail(tt)

    def mixer(self):
        S = self.S
        OFF_Q = 0
        self.qaug = [self.carve(OFF_Q + 4096 * e, 4096, BF16) for e in range(2)]
        self.kaug = [self.carve(8192 + 4096 * e, 4096, BF16) for e in range(2)]
        self.V = self.carve(16384, 8192, BF16, "p (e s d) -> p e s d", e=2, s=16)
        self.LtE = [self.carve(24576, 16384, BF16, "p (a b) -> p a b", b=512),
                    self.carve(81920, 16384, BF16, "p (a b) -> p a b", b=512)]
        self.et = self.carve(40960, 12288, F32, "p (a b) -> p a b", a=2)
        self.At = self.carve(53248, 6144, BF16, "p (a b) -> p a b", a=2)
        self.wqkv = self.carve(102400, 6144, BF16, "p (n j k f) -> p n j k f", n=1, j=3, k=8)
        self.yT = [self.carve(65536, 16384, BF16, "p (a b) -> p a b", b=T),
                   self.carve(81920, 16384, BF16, "p (a b) -> p a b", b=T)]
        self.Pc = self.carve(98304, 4096, BF16)
        self.At2 = self.carve(59392, 6144, BF16, "p (a b) -> p a b", a=2)
        src0 = self.d_win[0:9:4].rearrange("c p x -> p c x")
        self.dma("pool", self.wqkv[:, 0].rearrange("p j k f -> p j (k f)"), src0, writes=[("wqkv", 0)])
        S.barrier()
        wf = self.wf[:].rearrange("p (k f) -> p k f", f=8)
        ef = self.carve(81920, 8192, F32)
        Pf = self.carve(90112, 8192, F32)
        r1 = ef
        Ps = self.carve(65536, 12288, BF16).rearrange("p (i t) -> p i t", i=3)
        self.dma("pool", self.wf[:], self.d_wf, writes=["wf"])
        S.add("dve", lambda v: v.tensor_scalar(out=self.negb[:], in0=self.bfor[:], scalar1=-1.0, scalar2=None, op0=ALU.mult),
              reads=["bfor"], writes=["negb"])
        for tt in range(4):
            b = self.bank(0, 4)
            for k in range(8):
                self.mm(self.psb(b, slice(0, 8)), wf[:, k, :], self.hT[:, k, tt * 512:(tt + 1) * 512], k == 0, k == 7,
                        reads=["wf", ("h", k, tt)], writes=[("ps", b)])
            self.act(ef[0:8, tt * 512:(tt + 1) * 512], self.psb(b, slice(0, 8)), AF.Exp, reads=[("ps", b), "negb"],
                     writes=["ef"], bias=self.negb[:], scale=-1.0)
        self.act(ef[0:8, :], ef[0:8, :], AF.Ln, reads=["ef"], writes=["ef"], bias=1.0)
        S.add("dve", lambda v: v.tensor_tensor_scan(out=Pf[0:8, :], data0=ef[0:8, :], data1=ef[0:8, :], initial=0.0,
                                                    op0=ALU.add, op1=ALU.bypass), reads=["ef"], writes=["Pf"])
        self.copy(Ps[0:8, 0, :], Pf[0:8, :], reads=["Pf"], writes=["Ps0"])
        self.tt(r1[0:8, :], Pf[0:8, :], Ps[0:8, 0, :], ALU.subtract, reads=["Pf", "Ps0", "ef"], writes=["r1"])
        self.copy(Ps[0:8, 1, :], r1[0:8, :], reads=["r1"], writes=["Ps1"])
        self.tt(r1[0:8, :], r1[0:8, :], Ps[0:8, 1, :], ALU.subtract, reads=["r1", "Ps1"], writes=["r1"])
        self.copy(Ps[0:8, 2, :], r1[0:8, :], reads=["r1"], writes=["Ps2"])
        for i3 in range(3):
            self.dma("sp", self.Pc[8 * i3:8 * i3 + 8, :], Ps[0:8, i3, :], reads=["Ps0", "Ps1", "Ps2"], writes=["Pc"])
        for e in range(2):
            A = 64 if e == 0 else 0
            self.dma("pool", self.kaug[e][A:A + 64, :], self.d_caug[0], writes=[("kaug", e)])
        for p in range(4):
            self.project(p, 0)
            self.emit_items(self.sb_pair_items(p))
        src1 = self.d_win[12:21:4].rearrange("c p x -> p c x")
        self.dma("pool", self.wqkv[:, 0].rearrange("p j k f -> p j (k f)"), src1, writes=[("wqkv", 0)])
        S.barrier()
        for e in range(2):
            A = 64 if e == 0 else 0
            self.dma("pool", self.qaug[e][A:A + 64, :], self.d_caug[1], writes=[("qaug", e)])
            self.dma("pool", self.kaug[e][A:A + 64, :], self.d_caug[2], writes=[("kaug", e)])
        S.add("pool", lambda g: g.memset(self.V[:, 0, :, 64:128], 1.0), writes=["V"])
        S.add("pool", lambda g: g.memset(self.V[:, 1, :, 0:64], 1.0), writes=["V"])
        for p in range(4):
            self.project(p, 1)
            if p == 0:
                wup_ = self.wup_tiles()
                self.dma("pool", wup_[0].rearrange("p k n -> p (k n)"), self.d_wupa, writes=["wupa"])
                self.dma("pool", wup_[1].rearrange("p k n -> p (k n)"), self.d_wupb, writes=["wupb"])
            fa, fb = self.fox_items(p, 0), self.fox_items(p, 1)
            zipped = []
            for ia, ib in zip(fa, fb):
                zipped += [ia, ib]
            self.emit_items(zipped)
        if self.dbg:
            for br in range(2):
                self.dma("sp", self.d_dbg[br], self.yT[br].rearrange("p a b -> p (a b)"),
                         reads=[("y", br, c, tt) for c in range(4) for tt in range(4)])
        S.barrier()
        self.out_stage()

    def project(self, p, br):
        buf = 0
        base = 12 * br
        src = self.d_win[base + p:base + p + 9:4].rearrange("c p x -> p c x")
        if p != 0:
            self.dma("pool", self.wqkv[:, buf].rearrange("p j k f -> p j (k f)"), src, writes=[("wqkv", buf)])
        for which, dst, scale in ((0, self.qaug, 0.125), (1, self.kaug, 1.0)):
            key = "qaug" if which == 0 else "kaug"
            for tt in range(4):
                b = self.bank(0, 4)
                for k in range(8):
                    self.mm(self.psb(b), self.wqkv[:, buf, which, k, :], self.hT[:, k, tt * 512:(tt + 1) * 512],
                            k == 0, k == 7, reads=[("wqkv", buf), ("h", k, tt)], writes=[("ps", b)])
                for e in range(2):
                    rows = slice(0, 64) if e == 0 else slice(64, 128)
                    self.act(dst[e][rows, tt * 512:(tt + 1) * 512], self.psb(b, rows), AF.Copy,
                             reads=[("ps", b)], writes=[(key, e)], scale=scale)
        for s4 in range(4):
            if br == 0:
                break
            b = self.bank(0, 4)
            for sb_ in range(4):
                blk = s4 * 4 + sb_
                for k in range(8):
                    self.mm(self.psb(b, c0=sb_ * 128, c1=(sb_ + 1) * 128), self.hT[:, k, blk * 128:(blk + 1) * 128],
                            self.wqkv[:, buf, 2, k, :], k == 0, k == 7,
                            reads=[("wqkv", buf), ("h", k, s4)], writes=[("ps", b)])
            pv = self.psb(b).rearrange("p (s d) -> p s d", d=128)
            if br == 0:
                self.copy(self.V[:, 0, s4 * 4:(s4 + 1) * 4, :], pv, reads=[("ps", b)], writes=["V"])
            else:
                self.copy(self.V[:, 0, s4 * 4:(s4 + 1) * 4, 0:64], pv[:, :, 0:64], reads=[("ps", b)], writes=["V"])
                self.copy(self.V[:, 1, s4 * 4:(s4 + 1) * 4, 64:128], pv[:, :, 64:128], reads=[("ps", b)], writes=["V"])
        if br == 1:
            for e in range(2):
                A = 64 if e == 0 else 0
                h = 2 * p + e
                self.dma("sp", self.qaug[e][A:A + 3, :], self.Pc[h:24:8, :], reads=["Pc"], writes=[("qaug", e)])
                self.dma("sp", self.kaug[e][A + 3:A + 6, :], self.Pc[h:24:8, :], reads=["Pc"], writes=[("kaug", e)])

    DG_OFF = {0: 0, 1: 512, 3: 896, 2: 1024}

    def batches(self, e, g, Lt):
        out = []
        kb = 0
        while kb < 4 * g:
            m = min(3, 4 * g - kb)
            segs = []
            for i in range(m):
                segs.append(dict(kb=kb + i, off=512 * i, w=512, qc=0, dg=False,
                                 lt=(Lt[:, kb + i, :] if Lt is not None else None)))
            lt_all = Lt[:, kb:kb + m, :].rearrange("p a b -> p (a b)") if Lt is not None else None
            out.append(dict(segs=segs, W=512 * m, lt_all=lt_all))
            kb += m
        ltf = Lt[:, 4 * g:4 * g + 4, :].rearrange("p a b -> p (a b)") if Lt is not None else None
        segs = []
        for j in range(4):
            off = self.DG_OFF[j]
            w = 512 - 128 * j
            segs.append(dict(kb=4 * g + j, off=off, w=w, qc=128 * j, dg=True,
                             lt=(ltf[:, off:off + w] if Lt is not None else None)))
        out.append(dict(segs=segs, W=1280, lt_all=(ltf[:, 0:1280] if Lt is not None else None)))
        return out

    def seg_ps(self, e, sg):
        c = 1536 * e + sg["off"]
        return self.ps[:, c:c + sg["w"]]

    def seg_key(self, e, sg):
        return ("ps", 3 * e + sg["off"] // 512)

    def attn_scores(self, e, g, bt, aug, mask):
        qa, ka = self.qaug[e], self.kaug[e]
        qrows = slice(0, 128) if aug else (slice(0, 64) if e == 0 else slice(64, 128))
        G0 = 512 * g
        msk = self.maskS if mask == 0 else self.maskC
        for sg in bt["segs"]:
            kb, dg = sg["kb"], sg["dg"]
            out = self.seg_ps(e, sg)
            key = self.seg_key(e, sg)
            tri = aug and mask == 0
            self.mm(out, ka[qrows, kb * 128:(kb + 1) * 128], qa[qrows, G0 + sg["qc"]:G0 + 512],
                    True, (not dg) and not tri, reads=[("kaug", e), ("qaug", e)], writes=[key])
            if tri:
                self.mm(out, self.triN, sg["lt"], False, not dg, reads=[("Lt", e, kb), "cm"], writes=[key])
            if dg:
                self.mm(out[:, 0:128], self.ident, msk, False, True, reads=["cm"], writes=[key])

    def head_oplist(self, items):
        real = self.S
        rec = Recorder()
        self.S = rec
        pend = []
        for first, pe1, act, pe2 in items:
            if first:
                for f in pend:
                    f()
                pend = []
            if pe1:
                pe1()
            for f in pend:
                f()
            pend = []
            if act:
                act()
            if pe2:
                pend.append(pe2)
        for f in pend:
            f()
        self.S = real
        return rec.ops

    def interleave(self, seqs, delay):
        lists = [self.head_oplist(it) for it in seqs]
        S = self.S
        ptr = [0] * len(lists)
        free = {e: 0.0 for e in ENGS}
        avail = {}
        lastrd = {}
        LAT = 0.12
        while True:
            best = None
            for h in range(len(lists)):
                if ptr[h] >= len(lists[h]):
                    continue
                eng, fn, rd, wr, dur = lists[h][ptr[h]]
                t = free[eng]
                for k in rd:
                    t = max(t, avail.get(k, 0.0) + LAT)
                for k in wr:
                    t = max(t, avail.get(k, 0.0) + LAT, lastrd.get(k, 0.0) + LAT)
                if best is None or t < best[0]:
                    best = (t, h)
            if best is None:
                break
            t, h = best
            eng, fn, rd, wr, dur = lists[h][ptr[h]]
            ptr[h] += 1
            S.add(eng, fn, reads=rd, writes=wr)
            end = t + dur
            free[eng] = end
            for k in wr:
                avail[k] = end
            for k in rd:
                lastrd[k] = max(lastrd.get(k, 0.0), end)

    def v_unit(self, s4, b):
        for sb_ in range(4):
            blk = s4 * 4 + sb_
            for k in range(8):
                self.mm(self.psb(b, c0=sb_ * 128, c1=(sb_ + 1) * 128), self.hT[:, k, blk * 128:(blk + 1) * 128],
                        self.wqkv[:, 0, 2, k, :], k == 0, k == 7,
                        reads=[("wqkv", 0), ("h", k, s4)], writes=[("ps", b)])
        pv = self.psb(b).rearrange("p (s d) -> p s d", d=128)
        self.copy(self.V[:, 0, s4 * 4:(s4 + 1) * 4, :], pv, reads=[("ps", b)], writes=[("V", s4)])

    def emit_items(self, items):
        pend = []
        for first, pe1, act, pe2 in items:
            if first:
                for f in pend:
                    f()
                pend = []
            if pe1:
                pe1()
            for f in pend:
                f()
            pend = []
            if act:
                act()
            if pe2:
                pend.append(pe2)
        for f in pend:
            f()

    def sb_pair_items(self, p):
        H = []
        for e in range(2):
            H.append(dict(
                e=e, qa=self.qaug[e], ka=self.kaug[e],
                qrows=slice(0, 64) if e == 0 else slice(64, 128),
                arows=slice(64, 128) if e == 0 else slice(0, 64),
                lorows=slice(96, 128) if e == 0 else slice(32, 64),
                Vh=self.V[:, 0, :, 64 * e:64 * e + 64],
                Lt=self.LtE[e], B=6 + e, zc=1536 * e))

        def ltkeys(e, g, sg):
            if sg["dg"]:
                return [("Lt", e, 4 * g + j) for j in range(4)]
            return [("Lt", e, sg["kb"])]

        def p1_items(g):
            G0 = 512 * g
            bts = [self.batches(e, g, H[e]["Lt"]) for e in range(2)]
            nb = len(bts[0])
            nk = 4 * g + 4
            out = []
            first_b = nb - 1 if g >= 1 else 0
            last_b = nb - 2 if g >= 1 else nb - 1
            for bi in range(nb):
                bt = [bts[0][bi], bts[1][bi]]
                W = bt[0]["W"]
                ns = len(bt[0]["segs"])
                kbs = [sg["kb"] for sg in bt[0]["segs"]]

                def pe1(bt=bt, ns=ns):
                    for si in range(ns):
                        for h in H:
                            e = h["e"]
                            sg = bt[e]["segs"][si]
                            kb = sg["kb"]
                            self.mm(self.seg_ps(e, sg), h["ka"][h["qrows"], kb * 128:(kb + 1) * 128],
                                    h["qa"][h["qrows"], G0 + sg["qc"]:G0 + 512], True, not sg["dg"],
                                    reads=[("kaug", e), ("qaug", e)], writes=[self.seg_key(e, sg)])
                        for h in H:
                            e = h["e"]
                            sg = bt[e]["segs"][si]
                            if sg["dg"]:
                                self.mm(self.seg_ps(e, sg)[:, 0:128], self.ident, self.maskS, False, True,
                                        reads=["cm"], writes=[self.seg_key(e, sg)])

                def a_exp(bt=bt, W=W):
                    banks = []
                    for e in range(2):
                        banks += sorted(set(self.seg_key(e, sg) for sg in bt[e]["segs"]))
                    src = self.ps[:, 0:3072].rearrange("p (e w) -> p e w", e=2)[:, :, 0:W]
                    self.act(self.et[:, :, 0:W], src, AF.Exp, reads=banks, writes=[("et", 0), ("et", 1)])

                def a_ln(bt=bt, W=W):
                    wk = []
                    for e in range(2):
                        for sg in bt[e]["segs"]:
                            wk += ltkeys(e, g, sg)
                    sg0 = bt[0]["segs"][0]
                    off = (4 * g if sg0["dg"] else sg0["kb"]) * 512
                    dst = bass.AP(self.R, 12288 + off, [[54272, 128], [28672, 2], [1, W]])
                    self.act(dst, self.et[:, :, 0:W], AF.Ln, reads=[("et", 0), ("et", 1), "Pc"],
                             writes=sorted(set(wk)), bias=1.0)

                def rmm(bt=bt, ns=ns, lastb=(bi == last_b), firstb=(bi == first_b)):
                    for si in range(ns):
                        for h in H:
                            e = h["e"]
                            sg = bt[e]["segs"][si]
                            kb = sg["kb"]
                            self.mm(self.psb(h["B"], h["arows"], sg["qc"], 512), self.lhsR[:, kb, :], sg["lt"],
                                    firstb and si == 0, lastb and si == ns - 1,
                                    reads=ltkeys(e, g, sg) + ["cm"], writes=[("ps", h["B"])])
                    if lastb:
                        for h in H:
                            e = h["e"]
                            qa = h["qa"]
                            self.copy(qa[h["arows"], G0:G0 + 512], self.psb(h["B"], h["arows"]), reads=[("ps", h["B"])],
                                      writes=[("qaug", e)])
                            self.tt(qa[h["lorows"], G0:G0 + 512], self.psb(h["B"], h["lorows"]), qa[h["lorows"], G0:G0 + 512],
                                    ALU.subtract, reads=[("ps", h["B"]), ("qaug", e)], writes=[("qaug", e)])

                its = [(False, pe1, a_exp, None), (False, None, a_ln, rmm)]
                if bi == 0:
                    its.insert(1, (False, (lambda g=g: self.v_unit(g, 2 if g % 2 == 0 else 5)), None, None))
                out.append((kbs, its))
            if g >= 1:
                return [out[-1]], out[:-1]
            return [], out

        def p2_items(g):
            G0 = 512 * g
            bts = [self.batches(e, g, H[e]["Lt"]) for e in range(2)]
            nb = len(bts[0])
            nk = 4 * g + 4
            out = []
            for bi in range(nb):
                bt = [bts[0][bi], bts[1][bi]]
                W = bt[0]["W"]
                ns = len(bt[0]["segs"])
                kbs = [sg["kb"] for sg in bt[0]["segs"]]

                def pe1(bt=bt):
                    for e in range(2):
                        h = H[e]
                        for sg in bt[e]["segs"]:
                            kb, dg = sg["kb"], sg["dg"]
                            out_ = self.seg_ps(e, sg)
                            key = self.seg_key(e, sg)
                            self.mm(out_, h["ka"][:, kb * 128:(kb + 1) * 128], h["qa"][:, G0 + sg["qc"]:G0 + 512],
                                    True, False, reads=[("kaug", e), ("qaug", e)], writes=[key])
                            self.mm(out_, self.triN, sg["lt"], False, not dg, reads=ltkeys(e, g, sg) + ["cm"], writes=[key])
                            if dg:
                                self.mm(out_[:, 0:128], self.ident, self.maskS, False, True, reads=["cm"], writes=[key])

                ab = self.p2cnt % 2
                self.p2cnt += 1
                AtB = self.At if ab == 0 else self.At2

                def a_exp(bt=bt, W=W, AtB=AtB, ab=ab):
                    for h in H:
                        e = h["e"]
                        banks = sorted(set(self.seg_key(e, sg) for sg in bt[e]["segs"]))
                        self.act(AtB[:, e, 0:W], self.ps[:, h["zc"]:h["zc"] + W], AF.Exp, reads=banks, writes=[("At", e, ab)])

                def pv(bt=bt, ns=ns, lastb=(bi == nb - 1), AtB=AtB, ab=ab):
                    for si in range(ns):
                        for h in H:
                            e = h["e"]
                            sg = bt[e]["segs"][si]
                            kb = sg["kb"]
                            self.mm(self.psb(h["B"], h["qrows"], sg["qc"], 512), h["Vh"][:, kb, :],
                                    AtB[:, e, sg["off"]:sg["off"] + sg["w"]], kb == 0, kb == nk - 1,
                                    reads=[("At", e, ab), ("V", kb // 4)], writes=[("ps", h["B"])])
                    if lastb:
                        for h in H:
                            e = h["e"]
                            self.copy(self.yT[0][h["qrows"], p, G0:G0 + 512], self.psb(h["B"], h["qrows"]),
                                      reads=[("ps", h["B"]), "Pc"], writes=[("y", 0, p, g)])

                out.append((kbs, [(bi == 0, pe1, a_exp, pv)]))
            return out

        items = []
        p1parts = [p1_items(g) for g in range(4)]
        for kbs, its in p1parts[0][1]:
            items += its
        for kbs, its in p1parts[1][0]:
            items += its
        for g in range(4):
            p2 = p2_items(g)
            p1 = p1parts[g + 1][1] if g < 3 else []
            consumed = set()
            j = 0
            for kbs, its in p2:
                items += its
                consumed.update(kbs)
                while j < len(p1) and set(p1[j][0]) <= consumed:
                    items += p1[j][1]
                    j += 1
            while j < len(p1):
                items += p1[j][1]
                j += 1
            if g + 2 <= 3:
                for kbs, its in p1parts[g + 2][0]:
                    items += its
        return items

    def sb_items(self, p, e):
        qa = self.qaug[e]
        qrows = slice(0, 64) if e == 0 else slice(64, 128)
        arows = slice(64, 128) if e == 0 else slice(0, 64)
        lorows = slice(96, 128) if e == 0 else slice(32, 64)
        Vh = self.V[:, 0, :, 64 * e:64 * e + 64]
        Lt = self.LtE[e]
        RB = YB = 6 + e
        zc = 1536 * e
        items = []
        for g in range(4):
            G0 = 512 * g
            bts = self.batches(e, g, Lt)
            nb = len(bts)
            nk = 4 * g + 4
            for bi in range(nb):
                bt = bts[bi]
                W = bt["W"]
                banks = sorted(set(self.seg_key(e, sg) for sg in bt["segs"]))
                ltkeys = [("Lt", e, sg["kb"]) for sg in bt["segs"]]

                def pe1(bt=bt, g=g):
                    self.attn_scores(e, g, bt, False, 0)

                def a_exp(W=W, banks=banks):
                    self.act(self.et[:, e, 0:W], self.ps[:, zc:zc + W], AF.Exp, reads=banks, writes=[("et", e)])

                def a_ln(bt=bt, W=W, ltkeys=ltkeys):
                    self.act(bt["lt_all"], self.et[:, e, 0:W], AF.Ln,
                             reads=[("et", e), "Pc"], writes=ltkeys, bias=1.0)

                def rmm(bt=bt, nk=nk, lastb=(bi == nb - 1), G0=G0):
                    for sg in bt["segs"]:
                        kb = sg["kb"]
                        self.mm(self.psb(RB, arows, sg["qc"], 512), self.lhsR[:, kb, :], sg["lt"], kb == 0, kb == nk - 1,
                                reads=[("Lt", e, kb), "cm"], writes=[("ps", 6 + e)])
                    if lastb:
                        self.copy(qa[arows, G0:G0 + 512], self.psb(RB, arows), reads=[("ps", 6 + e)], writes=[("qaug", e)])
                        self.tt(qa[lorows, G0:G0 + 512], self.psb(RB, lorows), qa[lorows, G0:G0 + 512], ALU.subtract,
                                reads=[("ps", 6 + e), ("qaug", e)], writes=[("qaug", e)])

                items.append((bi == 0, pe1, a_exp, None))
                items.append((False, None, a_ln, rmm))
            for bi in range(nb):
                bt = bts[bi]
                W = bt["W"]
                banks = sorted(set(self.seg_key(e, sg) for sg in bt["segs"]))

                def pe1(bt=bt, g=g):
                    self.attn_scores(e, g, bt, True, 0)

                def a_exp(W=W, banks=banks):
                    self.act(self.At[:, e, 0:W], self.ps[:, zc:zc + W], AF.Exp, reads=banks, writes=[("At", e)])

                def pv(bt=bt, nk=nk, lastb=(bi == nb - 1), G0=G0, g=g):
                    for sg in bt["segs"]:
                        kb = sg["kb"]
                        self.mm(self.psb(YB, qrows, sg["qc"], 512), Vh[:, kb, :], self.At[:, e, sg["off"]:sg["off"] + sg["w"]],
                                kb == 0, kb == nk - 1, reads=[("At", e), "V"], writes=[("ps", 6 + e)])
                    if lastb:
                        self.copy(self.yT[0][qrows, p, G0:G0 + 512], self.psb(YB, qrows), reads=[("ps", 6 + e), "Pc"],
                                  writes=[("y", 0, p, g)])

                items.append((bi == 0, pe1, a_exp, pv))
        return items

    def fox_items(self, p, e):
        qrows = slice(0, 64) if e == 0 else slice(64, 128)
        arows = slice(64, 128) if e == 0 else slice(0, 64)
        Ve = self.V[:, e]
        rd = self.et
        YB = 6 + e
        zc = 1536 * e
        items = []
        for g in range(4):
            G0 = 512 * g
            bts = self.batches(e, g, None)
            nb = len(bts)
            nk = 4 * g + 4
            for bi in range(nb):
                bt = bts[bi]
                W = bt["W"]
                banks = sorted(set(self.seg_key(e, sg) for sg in bt["segs"]))

                def pe1(bt=bt, g=g):
                    self.attn_scores(e, g, bt, True, 1)

                def a_exp(W=W, banks=banks):
                    self.act(self.At[:, e, 0:W], self.ps[:, zc:zc + W], AF.Exp, reads=banks, writes=[("At", e)])

                def pv(bt=bt, nk=nk, lastb=(bi == nb - 1), G0=G0, g=g):
                    for sg in bt["segs"]:
                        kb = sg["kb"]
                        self.mm(self.psb(YB, slice(0, 128), sg["qc"], 512), Ve[:, kb, :],
                                self.At[:, e, sg["off"]:sg["off"] + sg["w"]], kb == 0, kb == nk - 1,
                                reads=[("At", e), "V"], writes=[("ps", 6 + e)])
                    if lastb:
                        ycp = rd[:, e, 512:1024]
                        self.copy(ycp, self.psb(YB), reads=[("ps", 6 + e)], writes=[("ycp", e)])
                        self.S.add("dve", lambda v, o=rd[qrows, e, 0:512], i_=ycp[arows, :]: v.reciprocal(out=o, in_=i_),
                                   reads=[("ycp", e)], writes=[("rd", e)], dur=3.4)
                        self.tt(self.yT[1][qrows, p, G0:G0 + 512], ycp[qrows, :], rd[qrows, e, 0:512], ALU.mult,
                                reads=[("ycp", e), ("rd", e)], writes=[("y", 1, p, g)])

                items.append((bi == 0, pe1, a_exp, pv))
        return items

    def wup_tiles(self):
        return [self.carve(24576, 8192, BF16, "p (k n) -> p k n", k=4), self.carve(32768, 8192, BF16, "p (k n) -> p k n", k=4)]

    def out_stage(self):
        S = self.S
        def mix(c):
            return self.carve(c * 4096, 4096, BF16) if c < 6 else self.carve(40960 + (c - 6) * 4096, 4096, BF16)
        wup = self.wup_tiles()
        sab = self.carve(49152, 4096, F32, "p (a b) -> p a b", b=512)
        wgt = self.carve(53248, 8192, BF16, "p (n g k f) -> p n g k f", n=2, g=2, k=8)
        wo = self.carve(98304, 4096, BF16, "p (n k f) -> p n k f", n=2, k=8)
        for c in range(8):
            buf = c % 2
            src = self.d_wgate[c:c + 9:8].rearrange("c p x -> p c x")
            self.dma("pool", wgt[:, buf].rearrange("p g k f -> p g (k f)"), src, writes=[("wgt", buf)])
            for tt in range(4):
                ts = slice(tt * 512, (tt + 1) * 512)
                bu = [self.bank(0, 8) for _ in range(4)]
                for br in range(2):
                    for k in range(4):
                        self.mm(self.psb(bu[br]), wup[br][:, k, c * 128:(c + 1) * 128], self.yT[br][:, k, ts], k == 0, k == 3,
                                reads=["wupa" if br == 0 else "wupb", ("y", br, k, tt)], writes=[("ps", bu[br])])
                for br in range(2):
                    for k in range(8):
                        self.mm(self.psb(bu[2 + br]), wgt[:, buf, br, k, :], self.hT[:, k, ts], k == 0, k == 7,
                                reads=[("wgt", buf), ("h", k, tt)], writes=[("ps", bu[2 + br])])
                for br in range(2):
                    self.act(sab[:, br, :], self.psb(bu[2 + br]), AF.Sigmoid, reads=[("ps", bu[2 + br]), "bgate"],
                             writes=[("sab", br)], bias=self.bgate[:, 8 * br + c:8 * br + c + 1])
                    self.tt(sab[:, br, :], sab[:, br, :], self.psb(bu[br]), ALU.mult,
                            reads=[("sab", br), ("ps", bu[br])], writes=[("sab", br)])
                self.tt(mix(c)[:, ts], sab[:, 0, :], sab[:, 1, :], ALU.add,
                        reads=[("sab", 0), ("sab", 1)], writes=[("mix", c, tt)])
        for c2 in range(8):
            buf = c2 % 2
            self.dma("pool", wo[:, buf].rearrange("p k f -> p (k f)"), self.d_wout[c2], writes=[("wo", buf)])
            for tt in range(4):
                ts = slice(tt * 512, (tt + 1) * 512)
                b = self.bank(0, 8)
                for k in range(8):
                    self.mm(self.psb(b), wo[:, buf, k, :], mix(k)[:, ts], k == 0, k == 7,
                            reads=[("wo", buf), ("mix", k, tt)], writes=[("ps", b)])
                xs = self.xT[:, c2, ts]
                self.tt(xs, self.psb(b), xs, ALU.add, reads=[("ps", b), ("x", c2, tt)], writes=[("x", c2, tt)])

def _consts():
    cm = np.zeros((128, 1664), np.float32)
    i = np.arange(128)
    cm[:, 0:128] = (i[:, None] == i[None, :])
    cm[:, 128:256] = np.where(i[:, None] >= i[None, :], NEG, 0.0)
    cm[:, 256:384] = np.where(i[:, None] > i[None, :], NEG, 0.0)
    cm[:, 384:512] = np.where(i[:, None] >= i[None, :], -1.0, 0.0)
    cm[:, 512:640] = 1.0 / 1024.0
    lr = np.zeros((16, 64), np.float32)
    for kb in range(16):
        for r in range(64):
            if (r < 16 and r < kb) or (32 <= r < 48 and (r - 32) < kb):
                lr[kb, r] = -1.0
    cm[:, 640:1664] = lr.reshape(1, 1024)
    caug = np.zeros((3, 64, 2048), np.float32)
    for kb in range(16):
        caug[0, kb, kb * 128:(kb + 1) * 128] = 1.0
        caug[0, 32 + kb, kb * 128:(kb + 1) * 128] = 1.0
    caug[1, 3:6, :] = 1.0
    caug[2, 0:3, :] = -1.0
    return cm, caug


def _chunked(w, kdim):
    K, N = w.shape
    a = w.reshape(K // 128, 128, N // 128, 128).transpose(2, 1, 0, 3)
    return np.ascontiguousarray(a).reshape(N // 128, 128, K)


def prepare_shared(inp):
    f = lambda a: np.asarray(a, dtype=np.float32)
    sh = {}
    ffn_w = ((1, inp["w_ffn1_gate"], inp["w_ffn1_up"], inp["w_ffn1_down"]),
             (2, inp["w_ffn2_gate"], inp["w_ffn2_up"], inp["w_ffn2_down"]))
    for i, w_g, w_u, w_d in ffn_w:
        wg = _chunked(f(w_g)[0], 1024)
        wu = _chunked(f(w_u)[0], 1024)
        sh["wg%d" % i] = np.ascontiguousarray(wg.reshape(11, 2, 128, 1024).transpose(0, 2, 1, 3)).reshape(11, 128, 2048)
        sh["wu%d" % i] = np.ascontiguousarray(wu.reshape(11, 2, 128, 1024).transpose(0, 2, 1, 3)).reshape(11, 128, 2048)
        sh["wd%d" % i] = np.ascontiguousarray(f(w_d)[0].reshape(22, 128, 1024))
    win = f(inp["w_in"])[0]
    sh["win"] = _chunked(win[:, :3072], 1024)
    sh["wf"] = np.ascontiguousarray(win[:, 3072:3080].reshape(8, 128, 8).transpose(1, 0, 2)).reshape(128, 64)
    sh["wgate"] = _chunked(f(inp["w_gate"])[0], 1024)
    sh["wupa"] = np.ascontiguousarray(f(inp["w_up_a"])[0].reshape(4, 128, 1024).transpose(1, 0, 2)).reshape(128, 4096)
    sh["wupb"] = np.ascontiguousarray(f(inp["w_up_b"])[0].reshape(4, 128, 1024).transpose(1, 0, 2)).reshape(128, 4096)
    sh["wout"] = _chunked(f(inp["w_out"])[0], 1024)
    gv = np.stack([f(inp["norm_ffn1"])[0], f(inp["norm_mix"])[0], f(inp["norm_ffn2"])[0], f(inp["norm_final"])], 0)
    sh["gvec"] = np.ascontiguousarray(gv.reshape(4, 8, 128).transpose(2, 0, 1)).reshape(128, 32)
    sh["bgate"] = np.ascontiguousarray(f(inp["b_gate"])[0].reshape(16, 128).T)
    sh["bfor"] = np.ascontiguousarray(f(inp["b_forget"])[0].reshape(8, 1))
    cm, caug = _consts()
    sh["cmat"] = cm
    sh["caug"] = caug
    return sh


_NC_CACHE = {}


def kernel(**inputs):
    x = np.asarray(inputs["x"], dtype=np.float32)
    B = x.shape[0]
    sh = prepare_shared(inputs)
    if "nc" not in _NC_CACHE:
        _NC_CACHE["nc"] = Builder(stage=3).build()
    nc = _NC_CACHE["nc"]
    in_maps = []
    for b in range(B):
        m = dict(sh)
        m["xT"] = np.ascontiguousarray(x[b].T)
        in_maps.append(m)
    res = run_bass_kernel_spmd(nc, in_maps, core_ids=list(range(B)))
    out = np.stack([np.asarray(res.results[b]["outT"]).T for b in range(B)], 0)
    return np.ascontiguousarray(out.astype(np.float32))
```
